# Optimizing a Trainium2 kernel written in Bass

```python
import math
import jax, jax.numpy as jnp
from jax import lax
import numpy as np

D_MODEL = 2048
BATCH = 4
SEQ = 8192
DEPTH = 1

GLA_HEADS = 4
GLA_HEAD_V = D_MODEL // 8
GLA_HEAD_K = GLA_HEAD_V // 2
GLA_DK = GLA_HEADS * GLA_HEAD_K
GLA_DV = GLA_HEADS * GLA_HEAD_V
GLA_GATE_RANK = 16
GLA_GATE_TAU = 16.0
GLA_CHUNK = 32

S5_GROUP_CH = 16
S5_WIDTH = D_MODEL // 4
S5_GROUPS = S5_WIDTH // S5_GROUP_CH
S5_STATE = 64
S5_DT_MIN = 1e-3
S5_DT_MAX = 1e-1

D_FF = ((8 * D_MODEL // 3 + 255) // 256) * 256

NORM_EPS = 1e-6
IN_WIDTHS = (GLA_DK, GLA_DK, GLA_DV, GLA_DV, GLA_GATE_RANK, S5_WIDTH, D_MODEL, D_MODEL)
D_IN = sum(IN_WIDTHS)

kernel_name = "hybrid_gla_s5_gated_block"


def _rmsnorm(x, g):
    xf = x.astype(jnp.float32)
    y = xf * lax.rsqrt(jnp.mean(xf * xf, axis=-1, keepdims=True) + NORM_EPS)
    return (y * g.astype(jnp.float32)).astype(x.dtype)


def _gla_branch(q, k, v, r, a_low, w_a2, b_a2, g_hn):
    f32 = jnp.float32
    bsz, seq, _ = q.shape
    n_chunks = seq // GLA_CHUNK
    z = (a_low @ w_a2 + b_a2).astype(f32)
    log_a = jax.nn.log_sigmoid(z) / GLA_GATE_TAU

    def chunks(t, hd):
        return t.astype(f32).reshape(bsz, n_chunks, GLA_CHUNK, GLA_HEADS, hd).transpose(0, 3, 1, 2, 4)

    qc = chunks(q, GLA_HEAD_K) * (GLA_HEAD_K ** -0.5)
    kc = chunks(k, GLA_HEAD_K)
    vc = chunks(v, GLA_HEAD_V)
    bc = jnp.cumsum(chunks(log_a, GLA_HEAD_K), axis=3)
    b_last = bc[..., -1:, :]
    q_dec = qc * jnp.exp(bc)
    k_intra = kc * jnp.exp(-bc)
    k_state = kc * jnp.exp(b_last - bc)

    causal = jnp.tril(jnp.ones((GLA_CHUNK, GLA_CHUNK), dtype=bool))
    scores = jnp.einsum('bhncd,bhnsd->bhncs', q_dec, k_intra)
    scores = jnp.where(causal, scores, 0.0)
    o_intra = jnp.einsum('bhncs,bhnsv->bhncv', scores, vc)

    def step(state, xs):
        qd, ks, vv, dl = xs
        o = jnp.einsum('bhcd,bhdv->bhcv', qd, state)
        state = dl[..., :, None] * state + jnp.einsum('bhcd,bhcv->bhdv', ks, vv)
        return state, o

    xs = (jnp.moveaxis(q_dec, 2, 0), jnp.moveaxis(k_state, 2, 0), jnp.moveaxis(vc, 2, 0),
          jnp.moveaxis(jnp.exp(b_last[..., 0, :]), 2, 0))
    s0 = jnp.zeros((bsz, GLA_HEADS, GLA_HEAD_K, GLA_HEAD_V), f32)
    _, o_inter = lax.scan(step, s0, xs)
    o = o_intra + jnp.moveaxis(o_inter, 0, 2)
    o = o.transpose(0, 2, 3, 1, 4).reshape(bsz, seq, GLA_HEADS, GLA_HEAD_V)
    o = o * lax.rsqrt(jnp.mean(o * o, axis=-1, keepdims=True) + NORM_EPS)
    o = o.reshape(bsz, seq, GLA_DV) * g_hn.astype(f32)
    return (jax.nn.silu(r.astype(f32)) * o).astype(q.dtype)


def _s5_branch(u, lam_re, lam_im, log_dt, b_re, b_im, c_re, c_im, d_skip, w_glu, b_glu):
    f32 = jnp.float32
    bsz, seq, _ = u.shape
    uf = u.astype(f32).reshape(bsz, seq, S5_GROUPS, S5_GROUP_CH)
    dt = jnp.exp(log_dt.astype(f32))[:, None]
    lr, li = lam_re.astype(f32), lam_im.astype(f32)
    mag = jnp.exp(lr * dt)
    abar_re = mag * jnp.cos(li * dt)
    abar_im = mag * jnp.sin(li * dt)
    den = lr * lr + li * li
    am1 = abar_re - 1.0
    f_re = ((am1 * lr + abar_im * li) / den)[..., None]
    f_im = ((abar_im * lr - am1 * li) / den)[..., None]
    br, bi = b_re.astype(f32), b_im.astype(f32)
    bbar_re = f_re * br - f_im * bi
    bbar_im = f_re * bi + f_im * br
    bu_re = jnp.einsum('btgc,gpc->tbgp', uf, bbar_re)
    bu_im = jnp.einsum('btgc,gpc->tbgp', uf, bbar_im)
    a_re = jnp.broadcast_to(abar_re, (seq, 1, S5_GROUPS, S5_STATE))
    a_im = jnp.broadcast_to(abar_im, (seq, 1, S5_GROUPS, S5_STATE))

    def combine(left, right):
        ar_l, ai_l, br_l, bi_l = left
        ar_r, ai_r, br_r, bi_r = right
        return (ar_r * ar_l - ai_r * ai_l,
                ar_r * ai_l + ai_r * ar_l,
                ar_r * br_l - ai_r * bi_l + br_r,
                ar_r * bi_l + ai_r * br_l + bi_r)

    _, _, xr, xi = lax.associative_scan(combine, (a_re, a_im, bu_re, bu_im), axis=0)
    y = (jnp.einsum('tbgp,gcp->btgc', xr, c_re.astype(f32))
         - jnp.einsum('tbgp,gcp->btgc', xi, c_im.astype(f32)))
    y = y.reshape(bsz, seq, S5_WIDTH) + d_skip.astype(f32) * u.astype(f32)
    h = jax.nn.gelu(y)
    out = h * jax.nn.sigmoid(h @ w_glu.astype(f32) + b_glu.astype(f32))
    return out.astype(u.dtype)


def setup_inputs(seed: int = 0) -> dict:
    key = jax.random.key(seed)
    ks = jax.random.split(key, 24)

    def nrm(k, shape, scale):
        return jax.random.normal(k, (DEPTH,) + shape, jnp.float32) * scale

    def gain(k, shape):
        return 1.0 + 0.02 * jax.random.normal(k, (DEPTH,) + shape, jnp.float32)

    n_idx = jnp.arange(S5_STATE, dtype=jnp.float32)
    lam_re = -0.5 + 0.01 * jax.random.normal(ks[5], (DEPTH, S5_GROUPS, S5_STATE), jnp.float32)
    lam_im = math.pi * n_idx + 0.01 * jax.random.normal(ks[6], (DEPTH, S5_GROUPS, S5_STATE), jnp.float32)
    log_dt = jax.random.uniform(ks[7], (DEPTH, S5_GROUPS), jnp.float32,
                                math.log(S5_DT_MIN), math.log(S5_DT_MAX))
    return {
        "x": jax.random.normal(ks[0], (BATCH, SEQ, D_MODEL), jnp.float32),
        "norm1_g": gain(ks[1], (D_MODEL,)),
        "w_in": nrm(ks[2], (D_MODEL, D_IN), D_MODEL ** -0.5),
        "w_a2": nrm(ks[3], (GLA_GATE_RANK, GLA_DK), GLA_GATE_RANK ** -0.5),
        "b_a2": nrm(ks[4], (GLA_DK,), 0.1),
        "gla_norm_g": gain(ks[8], (GLA_DV,)),
        "lam_re": lam_re,
        "lam_im": lam_im,
        "log_dt": log_dt,
        "s5_b_re": nrm(ks[9], (S5_GROUPS, S5_STATE, S5_GROUP_CH), (2 * S5_GROUP_CH) ** -0.5),
        "s5_b_im": nrm(ks[10], (S5_GROUPS, S5_STATE, S5_GROUP_CH), (2 * S5_GROUP_CH) ** -0.5),
        "s5_c_re": nrm(ks[11], (S5_GROUPS, S5_GROUP_CH, S5_STATE), (2 * S5_STATE) ** -0.5),
        "s5_c_im": nrm(ks[12], (S5_GROUPS, S5_GROUP_CH, S5_STATE), (2 * S5_STATE) ** -0.5),
        "s5_d": nrm(ks[13], (S5_WIDTH,), 1.0),
        "w_glu": nrm(ks[14], (S5_WIDTH, S5_WIDTH), S5_WIDTH ** -0.5),
        "b_glu": nrm(ks[15], (S5_WIDTH,), 0.02),
        "w_branch_a": nrm(ks[16], (GLA_DV, D_MODEL), GLA_DV ** -0.5),
        "w_branch_b": nrm(ks[17], (S5_WIDTH, D_MODEL), S5_WIDTH ** -0.5),
        "w_out": nrm(ks[18], (D_MODEL, D_MODEL), D_MODEL ** -0.5),
        "norm2_g": gain(ks[19], (D_MODEL,)),
        "w_ffn_in": nrm(ks[20], (D_MODEL, 2 * D_FF), D_MODEL ** -0.5),
        "w_ffn_out": nrm(ks[21], (D_FF, D_MODEL), D_FF ** -0.5),
        "final_norm_g": gain(ks[22], (D_MODEL,)),
    }


def reference(x, norm1_g, w_in, w_a2, b_a2, gla_norm_g, lam_re, lam_im, log_dt,
              s5_b_re, s5_b_im, s5_c_re, s5_c_im, s5_d, w_glu, b_glu,
              w_branch_a, w_branch_b, w_out, norm2_g, w_ffn_in, w_ffn_out, final_norm_g):
    splits = np.cumsum(IN_WIDTHS)[:-1].tolist()
    for l in range(DEPTH):
        h = _rmsnorm(x, norm1_g[l])
        proj = h @ w_in[l]
        q, k, v, r, a_low, u, gate_a, gate_b = jnp.split(proj, splits, axis=-1)
        o_a = _gla_branch(q, k, v, r, a_low, w_a2[l], b_a2[l], gla_norm_g[l])
        o_b = _s5_branch(u, lam_re[l], lam_im[l], log_dt[l], s5_b_re[l], s5_b_im[l],
                         s5_c_re[l], s5_c_im[l], s5_d[l], w_glu[l], b_glu[l])
        mix = (jax.nn.sigmoid(gate_a) * (o_a @ w_branch_a[l])
               + jax.nn.sigmoid(gate_b) * (o_b @ w_branch_b[l]))
        x = x + mix @ w_out[l]
        h = _rmsnorm(x, norm2_g[l])
        gate, up = jnp.split(h @ w_ffn_in[l], 2, axis=-1)
        x = x + (jax.nn.silu(gate) * up) @ w_ffn_out[l]
    return _rmsnorm(x, final_norm_g)
```

```python
import itertools
import math
import numpy as np
import concourse.bass as bass
import concourse.mybir as mybir
from concourse.bass_utils import run_bass_kernel_spmd

F32 = mybir.dt.float32
BF16 = mybir.dt.bfloat16
I32 = mybir.dt.int32
U8 = mybir.dt.uint8
AF = mybir.ActivationFunctionType
ALU = mybir.AluOpType

D = 2048
DIN = 7696
DFF = 5632
SEQH = 4096
NT = 512
NB = NT // 128
OFF_Q, OFF_K, OFF_V, OFF_R, OFF_AL, OFF_U, OFF_GA, OFF_GB = 0, 512, 1024, 2048, 3072, 3088, 3600, 5648
PAGE = 256
ENG_NAMES = ("pe", "act", "dve", "pool", "sp")
TWO_PI = 2.0 * math.pi
PI_SAFE = 3.1415925

CP_CMASK = 0
CP_G1 = CP_CMASK + NT
CP_G2 = CP_G1 + 16
CP_GHN = CP_G2 + 16
CP_BA2 = CP_GHN + 8
CP_DS5 = CP_BA2 + 4
CP_BGLU = CP_DS5 + 4
CP_N = CP_BGLU + 4
CT_ID = 0
CT_MASK = 128
CT_WA2 = 256
CT_LRE = CT_WA2 + 512
CT_LIM = CT_LRE + 16
CT_LDT = CT_LIM + 16
CT_BRE = CT_LDT + 16
CT_BIM = CT_BRE + 256
CT_CRE = CT_BIM + 256
CT_CIM = CT_CRE + 512
CT_N = CT_CIM + 512


class View:
    __slots__ = ("ap", "keys")

    def __init__(self, ap, keys):
        self.ap = ap
        self.keys = keys

    def bc(self, ap):
        return View(ap, self.keys)


class Buf:
    def __init__(self, space, base, off, shape, dt, es, raw=False):
        self.space = space
        self.off = off
        self.shape = tuple(int(s) for s in shape)
        self.es = es
        n = int(np.prod(self.shape))
        if raw:
            ap = base
        else:
            ap = base[:, off:off + n * es].bitcast(dt)
        if len(self.shape) > 1:
            names = ["d%d" % i for i in range(len(self.shape))]
            kw = {nm: s for nm, s in zip(names[:-1], self.shape[:-1])}
            ap = ap.rearrange("p (%s) -> p %s" % (" ".join(names), " ".join(names)), **kw)
        self.ap = ap
        st = [es] * len(self.shape)
        for i in range(len(self.shape) - 2, -1, -1):
            st[i] = st[i + 1] * self.shape[i + 1]
        self.strides = st

    def v(self, *idx, parts=None):
        shape = self.shape
        idx = list(idx) + [slice(None)] * (len(shape) - len(idx))
        rng = []
        for i, s in zip(idx, shape):
            if isinstance(i, slice):
                a, b, _ = i.indices(s)
                rng.append((a, b))
            else:
                rng.append((i, i + 1))
        L = len(shape) - 1
        while L > 0 and rng[L] == (0, shape[L]):
            L -= 1
        pages = set()
        st = self.strides
        for combo in itertools.product(*[range(a, b) for a, b in rng[:L]]):
            base = self.off
            for c, s_ in zip(combo, st):
                base += c * s_
            lo = base + rng[L][0] * st[L]
            hi = base + rng[L][1] * st[L]
            pages.update(range(lo // PAGE, (hi - 1) // PAGE + 1))
        sp = self.space
        if sp == "p":
            keys = (("p", self.off // 2048),)
        else:
            keys = tuple((sp, p) for p in pages)
        psl = slice(None) if parts is None else slice(parts[0], parts[1])
        return View(self.ap[(psl,) + tuple(idx)], keys)


class Op:
    __slots__ = ("eng", "fn", "rk", "wk", "dma", "idx", "waits", "signal", "count", "dsem", "dval", "wl", "inc", "dq")


class Prog:
    def __init__(self):
        self.ops = []

    def op(self, eng, fn, reads=(), writes=(), dma=False, dq=None):
        o = Op()
        o.eng = eng
        o.dq = dq if dq is not None else eng
        o.fn = fn
        rk = []
        for r in reads:
            if isinstance(r, View):
                rk.extend(r.keys)
            elif r is not None:
                rk.append(r)
        wk = []
        for w in writes:
            if isinstance(w, View):
                wk.extend(w.keys)
            elif w is not None:
                wk.append(w)
        for k in rk:
            if isinstance(k, tuple) and k[0] == "p" and k not in wk:
                wk.append(k)
        o.rk = rk
        o.wk = wk
        o.dma = dma
        o.idx = len(self.ops)
        o.waits = []
        o.signal = False
        o.count = 0
        o.dsem = None
        o.dval = 0
        self.ops.append(o)
        return o

    def analyze(self):
        ops = self.ops
        last_writer = {}
        readers = {}
        for o in ops:
            deps = {}
            for k in o.rk:
                w = last_writer.get(k)
                if w is not None:
                    deps[w] = True
            for k in o.wk:
                w = last_writer.get(k)
                if w is not None and w not in deps:
                    deps[w] = False
                rd = readers.get(k)
                if rd:
                    for r in rd.values():
                        if r not in deps:
                            deps[r] = False
            deps.pop(o.idx, None)
            for d, raw in deps.items():
                p = ops[d]
                if p.dma:
                    o.waits.append(d)
                    continue
                if (not o.dma) and p.eng == o.eng:
                    if o.eng == "pe":
                        continue
                o.waits.append(d)
                p.signal = True
            rkey = ("D", o.idx) if o.dma else o.eng
            for k in o.rk:
                rd = readers.get(k)
                if rd is None:
                    readers[k] = {rkey: o.idx}
                else:
                    rd[rkey] = o.idx
            for k in o.wk:
                last_writer[k] = o.idx
                readers[k] = None

    def plan(self, sems, dma_sems):
        ops = self.ops
        counts = {e: 0 for e in ENG_NAMES}
        for o in ops:
            if (not o.dma) and o.signal:
                counts[o.eng] += 1
                o.count = counts[o.eng]
        rr = {q: 0 for q in dma_sems}
        dvals = {q: [0] * len(dma_sems[q]) for q in dma_sems}
        seen = {e: {} for e in ENG_NAMES}
        self.n_wait = 0
        self.per_eng = {e: [] for e in ENG_NAMES}
        for o in ops:
            e = o.eng
            o.wl = []
            o.inc = None
            sd = seen[e]

            def do_wait(sem, val, key):
                if sd.get(key, 0) >= val:
                    return
                sd[key] = val
                o.wl.append((sem, val))
                self.n_wait += 1

            if o.dma:
                q = o.dq
                i = rr[q]
                rr[q] = (i + 1) % len(dma_sems[q])
                if dvals[q][i] > 0:
                    do_wait(dma_sems[q][i], dvals[q][i], ("d", q, i))
                dvals[q][i] += 16
                o.dsem = (q, i)
                o.dval = dvals[q][i]
                o.inc = (dma_sems[q][i], 16)
            elif o.signal:
                o.inc = (sems[e], 1)
            for d in o.waits:
                p = ops[d]
                if p.dma:
                    pe_, pi = p.dsem
                    do_wait(dma_sems[pe_][pi], p.dval, ("d", pe_, pi))
                else:
                    do_wait(sems[p.eng], p.count, ("e", p.eng))
            self.per_eng[e].append(o)

    def emit_engine(self, e, h):
        for o in self.per_eng[e]:
            for sem, val in o.wl:
                h.wait_ge(sem, val)
            inst = o.fn(h)
            if o.inc is not None:
                inst.then_inc(o.inc[0], o.inc[1])


class StopBuild(Exception):
    pass


class Builder:
    stop_at = 10 ** 9

    def ck(self, n):
        if n >= self.stop_at:
            raise StopBuild()

    def __init__(self, nc, n_warm, n_main, dbg):
        self.nc = nc
        self.P = Prog()
        self.n_warm = n_warm
        self.n_main = n_main
        self.dbg = dbg
        self.dbg_out = {}
        self.bank_rr = 0
        self.bank_held = set()
        self.slot_rr = 0

    def mm(self, out, lhsT, rhs, start=True, stop=True, tp=None):
        kw = {} if tp is None else {"tile_position": tp}
        self.P.op("pe", lambda h: h.matmul(out.ap, lhsT.ap, rhs.ap, start=start, stop=stop, **kw),
                  reads=[lhsT, rhs], writes=[out])

    def tr(self, out, in_):
        ident = self.ident
        self.P.op("pe", lambda h: h.transpose(out.ap, in_.ap, ident.ap), reads=[in_, ident], writes=[out])

    def act(self, out, in_, func, scale=None, bias=None, accum=None, extra_writes=()):
        reads = [in_]
        kw = {}
        if isinstance(scale, View):
            reads.append(scale)
            kw["scale"] = scale.ap
        elif scale is not None:
            kw["scale"] = float(scale)
        if isinstance(bias, View):
            reads.append(bias)
            kw["bias"] = bias.ap
        elif bias is not None:
            kw["bias"] = float(bias)
        writes = [out] + list(extra_writes)
        if accum is not None:
            writes.append(accum)
            kw["accum_out"] = accum.ap
        self.P.op("act", lambda h: h.activation(out=out.ap, in_=in_.ap, func=func, **kw), reads=reads, writes=writes)

    def tt(self, eng, out, a, b, op, extra_writes=()):
        self.P.op(eng, lambda h: h.tensor_tensor(out=out.ap, in0=a.ap, in1=b.ap, op=op), reads=[a, b],
                  writes=[out] + list(extra_writes))

    def ts(self, eng, out, a, s1, op0, s2=None, op1=None):
        reads = [a]
        v1 = s1
        if isinstance(s1, View):
            reads.append(s1)
            v1 = s1.ap
        v2 = s2
        if isinstance(s2, View):
            reads.append(s2)
            v2 = s2.ap
        if op1 is None:
            self.P.op(eng, lambda h: h.tensor_scalar(out=out.ap, in0=a.ap, scalar1=v1, scalar2=None, op0=op0),
                      reads=reads, writes=[out])
        else:
            self.P.op(eng, lambda h: h.tensor_scalar(out=out.ap, in0=a.ap, scalar1=v1, scalar2=v2, op0=op0, op1=op1),
                      reads=reads, writes=[out])

    def stt(self, out, a, s, b, op0, op1):
        reads = [a, b]
        sv = s
        if isinstance(s, View):
            reads.append(s)
            sv = s.ap
        self.P.op("dve", lambda h: h.scalar_tensor_tensor(out=out.ap, in0=a.ap, scalar=sv, in1=b.ap, op0=op0, op1=op1),
                  reads=reads, writes=[out])

    def scan(self, out, d0, d1, init):
        reads = [d0, d1]
        iv = init
        if isinstance(init, View):
            reads.append(init)
            iv = init.ap
        self.P.op("dve", lambda h: h.tensor_tensor_scan(out=out.ap, data0=d0.ap, data1=d1.ap, initial=iv,
                                                       op0=ALU.mult, op1=ALU.add), reads=reads, writes=[out])

    def cp(self, eng, out, in_, extra_writes=()):
        if eng == "act":
            self.act(out, in_, AF.Identity, extra_writes=extra_writes)
        else:
            self.P.op(eng, lambda h: h.tensor_copy(out=out.ap, in_=in_.ap), reads=[in_], writes=[out])

    def memset(self, eng, out, val):
        self.P.op(eng, lambda h: h.memset(out.ap, val), writes=[out])

    def dma(self, q, out_ap, in_ap, reads, writes, dq=None):
        self.P.op(q, lambda h: h.dma_start(out=out_ap, in_=in_ap), reads=reads, writes=writes, dma=True, dq=dq)

    def sb(self, off, shape, dt):
        es = {F32: 4, BF16: 2, I32: 4}[dt]
        n = int(np.prod(shape)) * es
        assert off % PAGE == 0, (off, shape)
        assert off + n <= self.arena_size, (off, n, self.arena_size)
        return Buf("s", self.arena, off, shape, dt, es)

    def bank(self, hold=False):
        for _ in range(8):
            b = self.bank_rr
            self.bank_rr = (b + 1) % 8
            if b not in self.bank_held:
                if hold:
                    self.bank_held.add(b)
                return b
        raise RuntimeError("all PSUM banks held")

    def release(self, b):
        self.bank_held.discard(b)

    def psf(self, b):
        return Buf("p", self.psum[b], b * 2048, (512,), F32, 4, raw=True)

    def psb(self, b):
        return Buf("p", self.psum[b].bitcast(BF16), b * 2048, (1024,), BF16, 2, raw=True)

    def wslot(self, nkc, ncols):
        s = self.slot_rr
        self.slot_rr = (s + 1) % 3
        assert nkc * ncols * 2 <= self.SLOT
        return self.sb(self.O_W + s * self.SLOT, (nkc, ncols), BF16)

    def wload(self, wname, kc0, nkc, c0, ncols):
        w_ap, plist = self.wb[wname]
        r_lo, r_hi = kc0 * 128, (kc0 + nkc) * 128
        keys = [k for (a, b, ra, rb_, k) in plist if a < c0 + ncols and c0 < b and ra < r_hi and r_lo < rb_]
        buf = self.wslot(nkc, ncols)
        src = w_ap[kc0 * 128:(kc0 + nkc) * 128, c0:c0 + ncols].rearrange("(kc p) n -> p kc n", p=128)
        dst = buf.v()
        self.dma("sp", dst.ap, src, reads=keys, writes=[dst])
        return buf

    def dump(self, name, view, shape, dt):
        if not self.dbg:
            return
        t = self.nc.dram_tensor("dbg_" + name, [128] + list(shape), dt, kind="ExternalOutput").ap()
        self.dbg_out[name] = t
        self.dma("pool", t, view.ap, reads=[view], writes=["dbg_" + name])
        self.final_reads.append("dbg_" + name)

    def build(self):
        nc = self.nc
        P = self.P
        self.final_reads = []
        x_d = nc.dram_tensor("x", [SEQH, D], F32, kind="ExternalInput").ap()
        xp_d = nc.dram_tensor("xp", [SEQH, D], F32, kind="ExternalInput").ap()
        out_d = nc.dram_tensor("out", [SEQH, D], F32, kind="ExternalOutput").ap()
        cstp_d = nc.dram_tensor("cstp", [128, CP_N], F32, kind="ExternalInput").ap()
        cstt_d = nc.dram_tensor("cstt", [128, CT_N], F32, kind="ExternalInput").ap()
        gfb_d = nc.dram_tensor("gfb", [128, D], F32, kind="ExternalInput").ap()
        def cols(c0, n, step=512):
            return [(c0 + i, min(step, n - i), None) for i in range(0, n, step)]

        def rows(K, N, step=512):
            return [(0, N, (r, min(step, K - r))) for r in range(0, K, step)]
        win_pieces = ([(OFF_AL, 528, None)] + cols(OFF_K, 512) + cols(OFF_V, 1024) + cols(OFF_Q, 512) + cols(OFF_R, 1024)
                      + cols(OFF_GA, 2048) + cols(OFF_GB, 2048))
        ffi_pieces = []
        for i in range(11):
            ffi_pieces += [(i * 512, 512, None), (DFF + i * 512, 512, None)]
        wspec = [("w_in", D, DIN, win_pieces), ("w_glu", 512, 512, cols(0, 512)), ("w_ba", 1024, D, cols(0, D, 1024)),
                 ("w_bb", 512, D, cols(0, D, 2048)), ("w_out", D, D, cols(0, D)), ("w_ffi", D, 2 * DFF, ffi_pieces),
                 ("w_ffo", DFF, D, rows(DFF, D))]
        self.wb = {}
        pending = []
        for name, K, N, pieces in wspec:
            src = nc.dram_tensor(name, [K, N], F32, kind="ExternalInput").ap()
            dst = nc.dram_tensor(name + "_b", [K, N], BF16, kind="Internal").ap()
            plist = []
            for i, (c0, cw, rr_) in enumerate(pieces):
                key = (name, i)
                if rr_ is None:
                    plist.append((c0, c0 + cw, 0, K, key))
                    pending.append((dst[:, c0:c0 + cw], src[:, c0:c0 + cw], key))
                else:
                    r0_, rn_ = rr_
                    plist.append((c0, c0 + cw, r0_, r0_ + rn_, key))
                    pending.append((dst[r0_:r0_ + rn_, c0:c0 + cw], src[r0_:r0_ + rn_, c0:c0 + cw], key))
            self.wb[name] = (dst, plist)

        def emit_casts(n, gate=None):
            for _ in range(min(n, len(pending))):
                d_, s_, key = pending.pop(0)
                self.dma("pool", d_, s_, reads=([gate] if gate is not None else []), writes=[key], dq="cast")
        emit_casts(4)

        o = 0

        def take(n):
            nonlocal o
            r = o
            o += (n + PAGE - 1) // PAGE * PAGE
            return r
        O_CSTP = take(CP_N * 4)
        O_IDENT = take(256)
        O_MASK = take(256)
        O_WA2 = take(1024)
        O_NBA2 = take(16)
        O_EC = take(16 * 128 * 4)
        O_ES = take(16 * 128 * 4)
        O_RHO = take(64)
        O_FG = take(5 * 3 * 16 * 4)
        O_BK = take(NB * 2 * 1024)
        O_CK = take(NB * 2 * 1024)
        O_CARRY = take(16 * 2 * 4)
        O_STATE = take(4 * 256 * 4)
        O_STATEB = take(4 * 256 * 2)
        O_SMALL = take(2048)
        O_WSM = take(512)
        O_H = take(16 * NT * 2)
        self.SLOT = 16 * 512 * 2
        self.O_W = take(3 * self.SLOT)
        O_MIX = take(16 * NT * 2)
        O_BIG = o
        self.arena_size = 212000 // PAGE * PAGE
        BIGSZ = self.arena_size - O_BIG
        assert BIGSZ >= 80 * 1024, BIGSZ

        ctx = []
        arena_t = nc.sbuf_tensor("arena", [128, self.arena_size], U8)
        self.arena = arena_t.__enter__()
        ctx.append(arena_t)
        self.psum = []
        for b in range(8):
            t = nc.psum_tensor("ps%d" % b, [128, 512], F32)
            self.psum.append(t.__enter__()[:])
            ctx.append(t)
        sem_ctx = {}
        for nm in ["s_pe", "s_act", "s_dve", "s_pool", "s_sp"] + ["dsp%d" % i for i in range(6)] + ["dpl%d" % i for i in range(4)] + ["dca%d" % i for i in range(2)]:
            c = nc.semaphore(nm)
            sem_ctx[nm] = c.__enter__()
            ctx.append(c)
        self.arena = self.arena[:]

        sb = self.sb
        cstp = sb(O_CSTP, (CP_N,), F32)
        self.ident = None
        ident_b = sb(O_IDENT, (128,), BF16)
        mask_b = sb(O_MASK, (128,), BF16)
        wa2_b = sb(O_WA2, (512,), BF16)
        nba2 = sb(O_NBA2, (4,), F32)
        Ec = sb(O_EC, (16, 128), F32)
        Es = sb(O_ES, (16, 128), F32)
        rho = sb(O_RHO, (16,), F32)
        FG = sb(O_FG, (5, 3, 16), F32)
        Bk = [[sb(O_BK + (k * 2 + c) * 1024, (4, 128), BF16) for c in range(2)] for k in range(NB)]
        Ck = [[sb(O_CK + (k * 2 + c) * 1024, (16, 32), BF16) for c in range(2)] for k in range(NB)]
        carry = sb(O_CARRY, (16, 2), F32)
        state = sb(O_STATE, (4, 256), F32)
        state_b = sb(O_STATEB, (4, 256), BF16)
        small = [sb(O_SMALL + i * 256, (64,), F32) for i in range(8)]
        R512 = sb(O_WSM, (16,), F32)
        Sel = sb(O_WSM + 256, (4,), F32)
        hT = sb(O_H, (16, NT), BF16)
        mixT = sb(O_MIX, (16, NT), BF16)
        self.ident = ident_b.v()

        def v3(buf, a=NB):
            v = buf.v()
            return v.bc(buf.ap.rearrange("p (a b) -> p a b", a=a))

        def ps3(pbuf, n, a):
            v = pbuf.v(slice(0, n))
            return v.bc(pbuf.ap[:, 0:n].rearrange("p (a b) -> p a b", a=a))


        cmask = cstp.v(slice(CP_CMASK, CP_CMASK + NT))
        g1c = cstp.v(slice(CP_G1, CP_G1 + 16))
        g2c = cstp.v(slice(CP_G2, CP_G2 + 16))
        ghn = cstp.v(slice(CP_GHN, CP_GHN + 8))
        ds5 = lambda ft: cstp.v(slice(CP_DS5 + ft, CP_DS5 + ft + 1))
        bglu = lambda ft: cstp.v(slice(CP_BGLU + ft, CP_BGLU + ft + 1))

        self.dma("sp", cstp.v().ap, cstp_d[:, :], reads=[], writes=[cstp.v()])
        ctt = sb(O_BIG, (CT_N,), F32)
        self.dma("sp", ctt.v().ap, cstt_d[:, :], reads=[], writes=[ctt.v()])
        tmpb = O_BIG + (CT_N * 4 + PAGE - 1) // PAGE * PAGE
        self.cp("dve", ident_b.v(), ctt.v(slice(CT_ID, CT_ID + 128)))
        self.cp("dve", mask_b.v(), ctt.v(slice(CT_MASK, CT_MASK + 128)))
        self.cp("dve", wa2_b.v(), ctt.v(slice(CT_WA2, CT_WA2 + 512)))
        self.ts("dve", nba2.v(), cstp.v(slice(CP_BA2, CP_BA2 + 4)), -1.0, ALU.mult)
        self.memset("dve", state.v(), 0.0)
        self.memset("dve", state_b.v(), 0.0)
        self.memset("dve", carry.v(), 0.0)

        try:
            self.ck(0)
        except StopBuild:
            self.stop_at = -1
        def t16(i):
            return sb(tmpb + i * 256, (16,), F32)
        lre = ctt.v(slice(CT_LRE, CT_LRE + 16))
        lim = ctt.v(slice(CT_LIM, CT_LIM + 16))
        ldt = ctt.v(slice(CT_LDT, CT_LDT + 16))
        dt_ = t16(0).v()
        self.act(dt_, ldt, AF.Exp)
        yv = t16(1).v()
        self.tt("dve", yv, lre, dt_, ALU.mult)
        pv = t16(2).v()
        self.ts("dve", pv, yv, 1.0 / 720.0, ALU.mult, 1.0 / 120.0, ALU.add)
        for cst in (1.0 / 24.0, 1.0 / 6.0, 0.5, 1.0, 1.0):
            self.tt("dve", pv, pv, yv, ALU.mult)
            self.ts("dve", pv, pv, cst, ALU.add)
        self.cp("dve", rho.v(), pv)
        th = t16(3).v()
        self.tt("dve", th, lim, dt_, ALU.mult)

        def sin_of(theta, slot, shift):
            a = t16(slot).v()
            if shift != 0.0:
                self.ts("dve", a, theta, shift, ALU.add)
            else:
                self.cp("dve", a, theta)
            t = t16(slot + 1).v()
            self.ts("dve", t, a, 1.0 / TWO_PI, ALU.mult)
            ki = sb(tmpb + (slot + 2) * 256, (16,), I32).v()
            self.cp("dve", ki, t)
            kf = t16(slot + 3).v()
            self.cp("dve", kf, ki)
            r = t16(slot + 1).v()
            self.stt(r, kf, -TWO_PI, a, ALU.mult, ALU.add)
            self.ts("dve", r, r, -PI_SAFE, ALU.max, PI_SAFE, ALU.min)
            s_ = t16(slot + 3).v()
            self.act(s_, r, AF.Sin)
            return s_
        sinv = sin_of(th, 4, 0.0)
        cosv = sin_of(th, 8, math.pi / 2.0)
        self.cp("dve", Ec.v(slice(None), slice(0, 1)), cosv.bc(cosv.ap.unsqueeze(2)))
        self.cp("dve", Es.v(slice(None), slice(0, 1)), sinv.bc(sinv.ap.unsqueeze(2)))
        tA = sb(tmpb + 16 * 256, (16, 64), F32)
        tB = sb(tmpb + 16 * 256 + 4096, (16, 64), F32)
        n = 1
        while n < 128:
            src_c = Ec.v(slice(None), slice(0, n))
            src_s = Es.v(slice(None), slice(0, n))
            mc_v = Ec.v(slice(None), slice(n - 1, n))
            ms_v = Es.v(slice(None), slice(n - 1, n))
            mc = mc_v.bc(mc_v.ap.broadcast_to([128, 16, n]))
            ms = ms_v.bc(ms_v.ap.broadcast_to([128, 16, n]))
            a_ = tA.v(slice(None), slice(0, n))
            b_ = tB.v(slice(None), slice(0, n))
            self.tt("dve", a_, src_c, mc, ALU.mult)
            self.tt("dve", b_, src_s, ms, ALU.mult)
            self.tt("dve", Ec.v(slice(None), slice(n, 2 * n)), a_, b_, ALU.subtract)
            a2 = tA.v(slice(None), slice(0, n))
            b2 = tB.v(slice(None), slice(0, n))
            self.tt("dve", a2, src_c, ms, ALU.mult)
            self.tt("dve", b2, src_s, mc, ALU.mult)
            self.tt("dve", Es.v(slice(None), slice(n, 2 * n)), a2, b2, ALU.add)
            n *= 2
        rc = t16(12).v()
        rs = t16(13).v()
        self.tt("dve", rc, rho.v(), cosv, ALU.mult)
        self.ts("dve", rc, rc, -1.0, ALU.add)
        self.tt("dve", rs, rho.v(), sinv, ALU.mult)
        den = t16(14).v()
        t2 = t16(15).v()
        self.tt("dve", den, lre, lre, ALU.mult)
        self.tt("dve", t2, lim, lim, ALU.mult)
        self.tt("dve", den, den, t2, ALU.add)
        self.P.op("dve", lambda h: h.reciprocal(out=den.ap, in_=den.ap), reads=[den], writes=[den])
        fre = t16(0).v()
        fim = t16(1).v()
        self.tt("dve", fre, rc, lre, ALU.mult)
        self.tt("dve", t2, rs, lim, ALU.mult)
        self.tt("dve", fre, fre, t2, ALU.add)
        self.tt("dve", fre, fre, den, ALU.mult)
        self.tt("dve", fim, rs, lre, ALU.mult)
        self.tt("dve", t2, rc, lim, ALU.mult)
        self.tt("dve", fim, fim, t2, ALU.subtract)
        self.tt("dve", fim, fim, den, ALU.mult)
        ta = t16(2).v()
        tb_ = t16(3).v()

        def fc(k):
            return FG.v(k, 0)

        def fs(k):
            return FG.v(k, 1)
        self.memset("dve", FG.v(), 0.0)
        self.memset("dve", fc(0), 1.0)
        e127c = Ec.v(slice(None), slice(127, 128))
        e127s = Es.v(slice(None), slice(127, 128))
        self.cp("dve", fc(1), e127c.bc(Ec.ap[:, :, 127]))
        self.cp("dve", fs(1), e127s.bc(Es.ap[:, :, 127]))
        for k in range(2, 5):
            self.tt("dve", ta, fc(k - 1), fc(1), ALU.mult)
            self.tt("dve", tb_, fs(k - 1), fs(1), ALU.mult)
            self.tt("dve", fc(k), ta, tb_, ALU.subtract)
            self.tt("dve", ta, fc(k - 1), fs(1), ALU.mult)
            self.tt("dve", tb_, fs(k - 1), fc(1), ALU.mult)
            self.tt("dve", fs(k), ta, tb_, ALU.add)
        self.ts("dve", FG.v(4, 2), fs(4), -1.0, ALU.mult)
        big0 = tmpb + 16 * 256 + 8192
        cre = ctt.v(slice(CT_CRE, CT_CRE + 512)).bc(ctt.ap[:, CT_CRE:CT_CRE + 512].rearrange("p (a b) -> p a b", a=16))
        cim = ctt.v(slice(CT_CIM, CT_CIM + 512)).bc(ctt.ap[:, CT_CIM:CT_CIM + 512].rearrange("p (a b) -> p a b", a=16))
        bpr = ctt.v(slice(CT_BRE, CT_BRE + 256)).bc(ctt.ap[:, CT_BRE:CT_BRE + 256].rearrange("p (a b) -> p a b", a=16))
        bpi = ctt.v(slice(CT_BIM, CT_BIM + 256)).bc(ctt.ap[:, CT_BIM:CT_BIM + 256].rearrange("p (a b) -> p a b", a=16))

        def b32(v):
            return v.bc(v.ap.unsqueeze(2).broadcast_to([128, 16, 32]))

        def b16(v):
            return v.bc(v.ap.unsqueeze(2).broadcast_to([128, 16, 16]))
        tC = sb(big0, (16, 32), F32)
        tD = sb(big0 + 2048, (16, 32), F32)
        cfr = sb(big0 + 4096, (16, 32), F32)
        cfi = sb(big0 + 6144, (16, 32), F32)
        tE = sb(big0 + 8192, (16, 16), F32)
        tF = sb(big0 + 9216, (16, 16), F32)
        tG = sb(big0 + 10240, (16, 16), F32)
        Mm = sb(big0 + 11264, (16, 32), BF16)
        self.tt("dve", tC.v(), cre, b32(fre), ALU.mult)
        self.tt("dve", tD.v(), cim, b32(fim), ALU.mult)
        self.tt("dve", cfr.v(), tC.v(), tD.v(), ALU.subtract)
        self.tt("dve", tC.v(), cre, b32(fim), ALU.mult)
        self.tt("dve", tD.v(), cim, b32(fre), ALU.mult)
        self.tt("dve", cfi.v(), tC.v(), tD.v(), ALU.add)
        def build_B(mc_, ms_, c, dstv, three_d=False):
            if c == 0:
                self.tt("dve", tE.v(), bpr, b16(mc_), ALU.mult)
                self.tt("dve", tF.v(), bpi, b16(ms_), ALU.mult)
                self.tt("dve", tG.v(), tE.v(), tF.v(), ALU.add)
            else:
                self.tt("dve", tE.v(), bpi, b16(mc_), ALU.mult)
                self.tt("dve", tF.v(), bpr, b16(ms_), ALU.mult)
                self.tt("dve", tG.v(), tE.v(), tF.v(), ALU.subtract)
            self.memset("dve", Mm.v(), 0.0)
            self.cp("dve", Mm.v(slice(None), slice(0, 16), parts=(0, 64)), tG.v(parts=(0, 64)))
            self.cp("dve", Mm.v(slice(None), slice(16, 32), parts=(64, 128)), tG.v(parts=(64, 128)))
            b = self.bank()
            pb = self.psb(b)
            for ft in range(4):
                src = Mm.v(slice(ft * 4, ft * 4 + 4))
                self.tr(pb.v(slice(ft * 128, (ft + 1) * 128)),
                        src.bc(Mm.ap[:, ft * 4:ft * 4 + 4, :].rearrange("p a b -> p (a b)")))
            if three_d:
                self.cp("dve", dstv, ps3(pb, 512, 4))
            else:
                self.cp("dve", dstv, pb.v(slice(0, 512)))

        for k in range(NB):
            self.tt("dve", tC.v(), cfr.v(), b32(fc(k)), ALU.mult)
            self.tt("dve", tD.v(), cfi.v(), b32(fs(k)), ALU.mult)
            self.tt("dve", Ck[k][0].v(), tC.v(), tD.v(), ALU.subtract)
            self.tt("dve", tC.v(), cfr.v(), b32(fs(k)), ALU.mult)
            self.tt("dve", tD.v(), cfi.v(), b32(fc(k)), ALU.mult)
            self.tt("dve", Ck[k][1].v(), tC.v(), tD.v(), ALU.add)
            for c in range(2):
                dstv = Bk[k][c].v().bc(Bk[k][c].ap.rearrange("p a b -> p (a b)"))
                build_B(fc(k), fs(k), c, dstv)

        O_B_ = O_BIG + NB * D * 4
        W_ALOW, W_UTOK, W_VT, W_STGX, W_STGN = O_B_, O_B_ + 1024, O_B_ + 5120, O_B_ + 13312, O_B_ + 21504
        W_TT0 = [O_B_ + 29696, O_B_ + 33792]
        W_HB = [O_B_ + 37888, O_B_ + 41984]
        TT0 = [sb(W_TT0[c], (16, 128), BF16) for c in range(2)]
        HB = [sb(W_HB[c], (4, NB, 128), BF16) for c in range(2)]
        if self.n_warm > 0:
            rp = [rho.v()]
            pw = sb(big0 + 12288, (8, 16), F32)
            for i in range(7):
                self.tt("dve", pw.v(i), rp[-1], rp[-1], ALU.mult)
                rp.append(pw.v(i))
            r128 = rp[7]
            r256 = pw.v(7)
            self.tt("dve", r256, r128, r128, ALU.mult)
            r384 = t16(4).v()
            self.tt("dve", r384, r256, r128, ALU.mult)
            self.tt("dve", R512.v(), r256, r256, ALU.mult)
            Dt = sb(big0 + 13312, (16, 128), F32)
            self.memset("dve", Dt.v(slice(None), slice(127, 128)), 1.0)
            n = 1
            i = 0
            while n < 128:
                srcv = Dt.v(slice(None), slice(128 - n, 128))
                mv = rp[i]
                mb = mv.bc(mv.ap.unsqueeze(2).broadcast_to([128, 16, n]))
                self.tt("dve", Dt.v(slice(None), slice(128 - 2 * n, 128 - n)), srcv, mb, ALU.mult)
                n *= 2
                i += 1
            T0b = sb(big0 + 21504, (16, 128), BF16)
            for c in range(2):
                if c == 0:
                    self.tt("dve", T0b.v(), Dt.v(), Ec.v(), ALU.mult)
                else:
                    self.stt(T0b.v(), Dt.v(), -1.0, Es.v(), ALU.mult, ALU.mult)
                for g in range(4):
                    b = self.bank()
                    pb = self.psb(b)
                    for j in range(4):
                        pr = g * 4 + j
                        self.tr(pb.v(slice(j * 128, (j + 1) * 128)), T0b.v(pr))
                    self.cp("dve", TT0[c].v(slice(g * 4, g * 4 + 4)), ps3(pb, 512, 4))
            hcs = [t16(5).v(), t16(6).v()]
            for k in range(NB):
                rk = [r384, r256, r128, None][k]
                if rk is None:
                    mc_, ms_ = fc(k), fs(k)
                else:
                    self.tt("dve", hcs[0], fc(k), rk, ALU.mult)
                    self.tt("dve", hcs[1], fs(k), rk, ALU.mult)
                    mc_, ms_ = hcs[0], hcs[1]
                for c in range(2):
                    dstv = HB[c].v(slice(None), k)
                    build_B(mc_, ms_, c, dstv.bc(HB[c].ap[:, :, k, :]), three_d=True)
            self.memset("dve", Sel.v(), 0.0)
            for q in range(3):
                self.memset("dve", Sel.v(slice(q, q + 1), parts=(q * 32, q * 32 + 32)), 1.0)
            self.memset("dve", Sel.v(slice(3, 4), parts=(96, 128)), 1.0)
        if self.dbg:
            self.dump("Ec", Ec.v(), (16, 128), F32)
            self.dump("Es", Es.v(), (16, 128), F32)
            self.dump("rho", rho.v(), (16,), F32)
            self.dump("FG", FG.v(), (5, 3, 16), F32)
            self.dump("Bk00", Bk[0][0].v(), (4, 128), BF16)
            self.dump("Bk11", Bk[1][1].v(), (4, 128), BF16)
            self.dump("Ck00", Ck[0][0].v(), (16, 32), BF16)
            self.dump("Ck21", Ck[2][1].v(), (16, 32), BF16)

        mask4 = mask_b.v().bc(mask_b.ap.unsqueeze(1).broadcast_to([128, 4, 128]))
        ghn_b = ghn.bc(cstp.ap[:, CP_GHN:CP_GHN + 8].unsqueeze(2).broadcast_to([128, 8, 128]))
        O_A = O_BIG
        ASZ = NB * D * 4
        O_B = O_BIG + ASZ
        BSZ = self.arena_size - O_B

        class Bump:
            def __init__(s_, base, size):
                s_.base = base
                s_.o = base
                s_.end = base + size

            def __call__(s_, n):
                r = s_.o
                s_.o += (n + PAGE - 1) // PAGE * PAGE
                assert s_.o <= s_.end, (s_.o - s_.base, s_.end - s_.base)
                return r

        def ec_b(pr):
            v = Ec.v(pr)
            return v.bc(Ec.ap[:, pr, :].unsqueeze(1).broadcast_to([128, NB, 128]))

        def es_b(pr):
            v = Es.v(pr)
            return v.bc(Es.ap[:, pr, :].unsqueeze(1).broadcast_to([128, NB, 128]))

        def norm_to_hT(gc, gcol0, dst, xt_, xn_):
            ss = small[0]
            lnv = small[1]
            rstd = small[2]
            for blk in range(NB):
                self.act(xn_[blk % 2].v(), xt_.v(blk), AF.Square, accum=ss.v(slice(blk, blk + 1)))
            self.act(lnv.v(slice(0, NB)), ss.v(slice(0, NB)), AF.Ln, scale=1.0 / D, bias=1e-6)
            self.act(rstd.v(slice(0, NB)), lnv.v(slice(0, NB)), AF.Exp, scale=-0.5)
            for blk in range(NB):
                xb = xn_[blk % 2]
                self.act(xb.v(), xt_.v(blk), AF.Identity, scale=rstd.v(slice(blk, blk + 1)))
                for kq in range(4):
                    b = self.bank()
                    pb = self.psb(b)
                    for j in range(4):
                        kc = kq * 4 + j
                        self.tr(pb.v(slice(j * 128, (j + 1) * 128)), xb.v(slice(kc * 128, (kc + 1) * 128)))
                    gsl = gc.bc(cstp.ap[:, gcol0 + kq * 4:gcol0 + kq * 4 + 4].unsqueeze(2).broadcast_to([128, 4, 128]))
                    dv = dst.v(slice(kq * 4, kq * 4 + 4), slice(blk * 128, (blk + 1) * 128))
                    self.tt("dve", dv, ps3(pb, 512, 4), gsl, ALU.mult)

        def prep_gen(src_d, t0, stg_x, stg_n):
            xs = sb(stg_x, (D,), F32)
            xn_ = [sb(stg_n + i * D * 2, (D,), BF16) for i in range(2)]
            ss = small[6]
            l2 = small[7]
            for blk in range(NB):
                self.dma("sp", xs.v().ap, src_d[t0 + blk * 128:t0 + (blk + 1) * 128, :], reads=[], writes=[xs.v()])
                xb = xn_[blk % 2]
                self.act(xb.v(), xs.v(), AF.Square, accum=ss.v(slice(blk, blk + 1)))
                self.act(l2.v(slice(blk, blk + 1)), ss.v(slice(blk, blk + 1)), AF.Ln, scale=1.0 / D, bias=1e-6)
                self.act(l2.v(slice(8 + blk, 9 + blk)), l2.v(slice(blk, blk + 1)), AF.Exp, scale=-0.5)
                self.act(xb.v(), xs.v(), AF.Identity, scale=l2.v(slice(8 + blk, 9 + blk)))
                for kq in range(4):
                    b = self.bank()
                    pb = self.psb(b)
                    for j in range(4):
                        kc = kq * 4 + j
                        self.tr(pb.v(slice(j * 128, (j + 1) * 128)), xb.v(slice(kc * 128, (kc + 1) * 128)))
                    gsl = g1c.bc(cstp.ap[:, CP_G1 + kq * 4:CP_G1 + kq * 4 + 4].unsqueeze(2).broadcast_to([128, 4, 128]))
                    dv = hT.v(slice(kq * 4, kq * 4 + 4), slice(blk * 128, (blk + 1) * 128))
                    self.tt("dve", dv, ps3(pb, 512, 4), gsl, ALU.mult)
                    yield

        STG_WARM = (W_STGX, W_STGN)
        STG_MAIN = (O_MIX, O_MIX + D * 4)

        def tile(src_d, t0, warm, first_main, nxt):
            A = Bump(O_A, ASZ)
            B = Bump(O_B, BSZ)
            self.ck(3 if warm else 22)

            A = Bump(O_A, ASZ)
            B = Bump(O_B, BSZ)
            if warm:
                alow = sb(W_ALOW, (NT,), BF16)
                utok = sb(W_UTOK, (NB, 512), BF16)
                uT = uTb = oAT = oBT = None
            else:
                alow = sb(B(NT * 2), (NT,), BF16)
                uT = sb(B(4 * NT * 4), (4, NT), F32)
                uTb = sb(B(4 * NT * 2), (4, NT), BF16)
                oAT = sb(B(8 * NT * 2), (8, NT), BF16)
                oBT = sb(B(4 * NT * 2), (4, NT), BF16)
            B_REST = B.o
            e_t = sb(A(NT * 4), (NT,), F32)
            l_t = sb(A(NT * 4), (NT,), F32)
            s_t = sb(A(NT * 4), (NT,), F32)
            E1 = sb(A(NT * 4), (NT,), F32)
            E2 = sb(A(NT * 4), (NT,), F32)
            E3 = sb(A(NT * 4), (NT,), F32)
            qd = sb(A(4 * NT * 2), (4, NT), BF16)
            ki = sb(A(4 * NT * 2), (4, NT), BF16)
            ks = sb(A(4 * NT * 2), (4, NT), BF16)
            kst = sb(A(NB * 512 * 2), (NB, 512), BF16)
            sT = [sb(A(512 * 2), (512,), BF16) for _ in range(2)]
            junk = sb(A(256 * 2), (256,), BF16)
            nsl = sb(A(256), (4, NB), F32)
            dl = sb(A(256), (4, NB), F32)
            if warm:
                vt = sb(W_VT, (NB, 1024), BF16)
                srt = oa = None
            else:
                vt = sb(B(NB * 1024 * 2), (NB, 1024), BF16)
                srt = sb(B(NB * 1024 * 2), (NB, 1024), BF16)
                oa = [sb(B(1024 * 2), (1024,), BF16)]

            wal = self.wload("w_in", 0, 16, OFF_AL, 32)
            wblk = self.wload("w_in", 0, 16, OFF_U, 512)
            self.ck(3.1 if warm else 22.1)
            b = self.bank()
            pa = self.psf(b)
            for kc in range(16):
                self.mm(pa.v(slice(0, NT), parts=(0, 32)), wal.v(kc), hT.v(kc), start=(kc == 0), stop=(kc == 15))
            self.ck(3.2 if warm else 22.2)
            self.cp("act", alow.v(parts=(0, 32)), pa.v(slice(0, NT), parts=(0, 32)))
            self.ck(3.3 if warm else 22.3)
            if warm:
                for blk in range(NB):
                    b = self.bank()
                    pu = self.psf(b)
                    for kc in range(16):
                        self.mm(pu.v(), hT.v(kc, slice(blk * 128, (blk + 1) * 128)), wblk.v(kc), start=(kc == 0), stop=(kc == 15))
                    self.cp("act", utok.v(blk), pu.v())
            else:
                for ft in range(4):
                    b = self.bank()
                    pu = self.psf(b)
                    for kc in range(16):
                        self.mm(pu.v(slice(0, NT)), wblk.v(kc, slice(ft * 128, (ft + 1) * 128)), hT.v(kc),
                                start=(kc == 0), stop=(kc == 15))
                    self.cp("act", uT.v(ft), pu.v(slice(0, NT)))
                    self.cp("act", uTb.v(ft), pu.v(slice(0, NT)))

            self.ck(5 if warm else 24)

            def gla_rest():
                wq = None if warm else self.wload("w_in", 0, 16, OFF_Q, 512)
                wk = self.wload("w_in", 0, 16, OFF_K, 512)
                for h in range(4):
                    b = self.bank()
                    pz = self.psf(b)
                    self.mm(pz.v(slice(0, NT)), wa2_b.v(slice(h * 128, (h + 1) * 128), parts=(0, 32)), alow.v(parts=(0, 32)))
                    self.act(e_t.v(), pz.v(slice(0, NT)), AF.Exp, scale=-1.0, bias=nba2.v(slice(h, h + 1)))
                    self.act(l_t.v(), e_t.v(), AF.Ln, bias=1.0)
                    self.scan(s_t.v(), cmask, l_t.v(), 0.0)
                    if not warm:
                        self.act(E1.v(), s_t.v(), AF.Exp, scale=-1.0 / 16.0)
                        self.act(E2.v(), s_t.v(), AF.Exp, scale=1.0 / 16.0)
                    sl_v = s_t.v()
                    self.ts("dve", nsl.v(h), sl_v.bc(s_t.ap[:, 127::128]), -1.0 / 16.0, ALU.mult)
                    for c in range(NB):
                        self.act(E3.v(slice(c * 128, (c + 1) * 128)), s_t.v(slice(c * 128, (c + 1) * 128)), AF.Exp,
                                 scale=1.0 / 16.0, bias=nsl.v(h, slice(c, c + 1)))
                    self.act(dl.v(h), nsl.v(h), AF.Exp)
                    if not warm:
                        b = self.bank()
                        pq = self.psf(b)
                        for kc in range(16):
                            self.mm(pq.v(slice(0, NT)), wq.v(kc, slice(h * 128, (h + 1) * 128)), hT.v(kc), start=(kc == 0), stop=(kc == 15))
                            if kc == 7:
                                yield
                        self.stt(qd.v(h), pq.v(slice(0, NT)), 128.0 ** -0.5, E1.v(), ALU.mult, ALU.mult)
                        yield
                    b = self.bank()
                    pk = self.psf(b)
                    for kc in range(16):
                        self.mm(pk.v(slice(0, NT)), wk.v(kc, slice(h * 128, (h + 1) * 128)), hT.v(kc), start=(kc == 0), stop=(kc == 15))
                        if kc == 7:
                            yield
                    if not warm:
                        self.tt("dve", ki.v(h), pk.v(slice(0, NT)), E2.v(), ALU.mult)
                    self.tt("dve", ks.v(h), pk.v(slice(0, NT)), E3.v(), ALU.mult, extra_writes=[("hd", t0, warm, h)])
                    emit_casts(1, gate=("hd", t0, warm, h))
                    yield


                for cb in range(2):
                    wv = self.wload("w_in", 0, 16, OFF_V + cb * 512, 512)
                    for blk in range(NB):
                        b = self.bank()
                        pv_ = self.psf(b)
                        for kc in range(16):
                            self.mm(pv_.v(), hT.v(kc, slice(blk * 128, (blk + 1) * 128)), wv.v(kc), start=(kc == 0), stop=(kc == 15))
                            if kc == 7:
                                yield
                        lastb = (blk == NB - 1)
                        self.cp("act", vt.v(blk, slice(cb * 512, (cb + 1) * 512)), pv_.v(),
                                extra_writes=([("vd", t0, warm, cb)] if lastb else []))
                        if lastb:
                            emit_casts(1, gate=("vd", t0, warm, cb))
                        yield
                if warm and nxt is not None:
                    yield from prep_gen(nxt[0], nxt[1], STG_WARM[0], STG_WARM[1])
                if not warm:
                    for cb in range(2):
                        wr_ = self.wload("w_in", 0, 16, OFF_R + cb * 512, 512)
                        for blk in range(NB):
                            b = self.bank()
                            pr_ = self.psf(b)
                            for kc in range(16):
                                self.mm(pr_.v(), hT.v(kc, slice(blk * 128, (blk + 1) * 128)), wr_.v(kc), start=(kc == 0), stop=(kc == 15))
                                if kc == 7:
                                    yield
                            self.act(srt.v(blk, slice(cb * 512, (cb + 1) * 512)), pr_.v(), AF.Silu)
                            yield
                for blk in range(NB):
                    b = self.bank()
                    pb = self.psb(b)
                    for h in range(4):
                        self.tr(pb.v(slice(h * 128, (h + 1) * 128)), ks.v(h, slice(blk * 128, (blk + 1) * 128)))
                    self.cp("act", kst.v(blk), pb.v(slice(0, 512)))
                yield
                ss4 = small[3]
                ln4 = small[4]
                rs4 = small[5]
                for c in range(NB):
                    csl = slice(c * 128, (c + 1) * 128)
                    if not warm:
                        b = self.bank()
                        psc = self.psf(b)
                        for h in range(4):
                            self.mm(psc.v(slice(h * 128, (h + 1) * 128)), ki.v(h, csl), qd.v(h, csl))
                        sTc = sT[c % 2]
                        self.tt("dve", v3(sTc, 4), ps3(psc, 512, 4), mask4, ALU.mult)
                        pos = []
                        posb = []
                        for hp in range(2):
                            b = self.bank(hold=True)
                            posb.append(b)
                            po = self.psf(b)
                            pos.append(po)
                            for hh in range(2):
                                h = hp * 2 + hh
                                ov = po.v(slice(hh * 256, (hh + 1) * 256))
                                self.mm(ov, sTc.v(slice(h * 128, (h + 1) * 128)), vt.v(c, slice(h * 256, (h + 1) * 256)), start=True, stop=False)
                                self.mm(ov, qd.v(h, csl), state_b.v(h), start=False, stop=True)
                        for h in range(4):
                            ov = pos[h // 2].v(slice((h % 2) * 256, (h % 2 + 1) * 256))
                            self.act(junk.v(), ov, AF.Square, accum=ss4.v(slice(h, h + 1)))
                        self.act(ln4.v(slice(0, 4)), ss4.v(slice(0, 4)), AF.Ln, scale=1.0 / 256.0, bias=1e-6)
                        self.act(rs4.v(slice(0, 4)), ln4.v(slice(0, 4)), AF.Exp, scale=-0.5)
                        oac = oa[0]
                        for h in range(4):
                            ov = pos[h // 2].v(slice((h % 2) * 256, (h % 2 + 1) * 256))
                            self.stt(oac.v(slice(h * 256, (h + 1) * 256)), ov, rs4.v(slice(h, h + 1)),
                                     srt.v(c, slice(h * 256, (h + 1) * 256)), ALU.mult, ALU.mult)
                        for b in posb:
                            self.release(b)
                        b = self.bank()
                        pb = self.psb(b)
                        for j in range(8):
                            self.tr(pb.v(slice(j * 128, (j + 1) * 128)), oac.v(slice(j * 128, (j + 1) * 128)))
                        self.tt("dve", oAT.v(slice(None), csl), ps3(pb, 1024, 8), ghn_b, ALU.mult)
                        yield
                    last = warm and (t0 + NT == SEQH) and (c == NB - 1)
                    for hp in range(2):
                        b = self.bank()
                        pkv = self.psf(b)
                        for hh in range(2):
                            h = hp * 2 + hh
                            self.mm(pkv.v(slice(hh * 256, (hh + 1) * 256)), kst.v(c, slice(h * 128, (h + 1) * 128)),
                                    vt.v(c, slice(h * 256, (h + 1) * 256)))
                        for hh in range(2):
                            h = hp * 2 + hh
                            self.stt(state.v(h), state.v(h), dl.v(h, slice(c, c + 1)), pkv.v(slice(hh * 256, (hh + 1) * 256)),
                                     ALU.mult, ALU.add)
                            if (not warm) or last:
                                self.cp("act", state_b.v(h), state.v(h))
                    yield
                if self.dbg and first_main:
                    self.dump("oAT", oAT.v(), (8, NT), BF16)
                    self.dump("hT", hT.v(), (16, NT), BF16)
                    self.dump("uT", uT.v(), (4, NT), F32)
                    self.dump("qd", qd.v(), (4, NT), BF16)
                    self.dump("ks", ks.v(), (4, NT), BF16)
                    self.dump("state", state.v(), (4, 256), F32)

            s5a = [sb(O_MIX + i * NT * 4, (NT,), F32) for i in range(6)]
            wrr = [s5a[0], s5a[2]]
            wim = [s5a[1], s5a[3]]
            tP = [s5a[4], s5a[5]]
            pbf = [sb(O_MIX + 6 * NT * 4 + i * NT * 2, (NT,), BF16) for i in range(4)]
            if warm:
                hS = None
                ctmp = sb(A(256), (16, 2), F32)
            else:
                hS = sb(B(4 * NT * 2), (4, NT), BF16)
                ctmp = sb(B(256), (16, 2), F32)

            def s5_gen():
                def issue_A(pr):
                    ft = pr // 4
                    r0 = (pr % 4) * 32
                    bR = self.bank(hold=True)
                    pR = self.psf(bR)
                    bI = self.bank(hold=True)
                    pI = self.psf(bI)
                    for k in range(NB):
                        ksl = slice(k * 128, (k + 1) * 128)
                        self.mm(pR.v(ksl), Bk[k][0].v(ft, parts=(r0, r0 + 32)), uTb.v(ft, ksl, parts=(r0, r0 + 32)), tp=(r0, 0))
                    for k in range(NB):
                        ksl = slice(k * 128, (k + 1) * 128)
                        self.mm(pI.v(ksl), Bk[k][1].v(ft, parts=(r0, r0 + 32)), uTb.v(ft, ksl, parts=(r0, r0 + 32)), tp=(r0, 0))
                    return (bR, pR, bI, pI)
                nxt = issue_A(0)
                by = None
                py = None
                for pr in range(16):
                    ft = pr // 4
                    if (not warm) and pr % 4 == 0:
                        by = self.bank(hold=True)
                        py = self.psf(by)
                    bR, pR, bI, pI = nxt
                    if pr + 1 < 16:
                        nxt = issue_A(pr + 1)
                    wr_ = wrr[pr % 2]
                    wi_ = wim[pr % 2]
                    pR3 = ps3(pR, NT, NB)
                    pI3 = ps3(pI, NT, NB)
                    self.tt("dve", v3(tP[0]), pR3, ec_b(pr), ALU.mult)
                    self.tt("dve", v3(tP[1]), pI3, es_b(pr), ALU.mult)
                    self.tt("dve", wr_.v(), tP[0].v(), tP[1].v(), ALU.add)
                    self.tt("dve", v3(tP[0]), pI3, ec_b(pr), ALU.mult)
                    self.tt("dve", v3(tP[1]), pR3, es_b(pr), ALU.mult)
                    self.tt("dve", wi_.v(), tP[0].v(), tP[1].v(), ALU.subtract)
                    self.release(bR)
                    self.release(bI)
                    yield
                    rv = rho.v(slice(pr, pr + 1))
                    rb = rv.bc(rv.ap.broadcast_to([128, NT]))
                    self.scan(wr_.v(), rb, wr_.v(), carry.v(pr, slice(0, 1)))
                    self.scan(wi_.v(), rb, wi_.v(), carry.v(pr, slice(1, 2)))
                    wre = wr_.v(slice(NT - 1, NT))
                    wie = wi_.v(slice(NT - 1, NT))
                    gc_ = FG.v(4, 0, slice(pr, pr + 1))
                    gs_ = FG.v(4, 1, slice(pr, pr + 1))
                    ngs = FG.v(4, 2, slice(pr, pr + 1))
                    self.act(ctmp.v(pr, slice(0, 1)), wie, AF.Identity, scale=ngs)
                    self.act(ctmp.v(pr, slice(1, 2)), wie, AF.Identity, scale=gc_)
                    self.act(carry.v(pr, slice(0, 1)), wre, AF.Identity, scale=gc_, bias=ctmp.v(pr, slice(0, 1)))
                    self.act(carry.v(pr, slice(1, 2)), wre, AF.Identity, scale=gs_, bias=ctmp.v(pr, slice(1, 2)))
                    if warm:
                        yield
                        continue
                    m0 = (pr % 4) * 32
                    self.tt("dve", v3(pbf[0]), v3(wr_), ec_b(pr), ALU.mult)
                    self.stt(v3(pbf[1]), v3(wi_), -1.0, es_b(pr), ALU.mult, ALU.mult)
                    self.stt(v3(pbf[2]), v3(wr_), -1.0, es_b(pr), ALU.mult, ALU.mult)
                    self.stt(v3(pbf[3]), v3(wi_), -1.0, ec_b(pr), ALU.mult, ALU.mult)
                    for k in range(NB):
                        ksl = slice(k * 128, (k + 1) * 128)
                        ov = py.v(ksl, parts=(m0, m0 + 32))
                        self.mm(ov, Ck[k][0].v(pr), pbf[0].v(ksl), start=True, stop=False, tp=(0, m0))
                        self.mm(ov, Ck[k][0].v(pr), pbf[1].v(ksl), start=False, stop=False, tp=(0, m0))
                        self.mm(ov, Ck[k][1].v(pr), pbf[2].v(ksl), start=False, stop=False, tp=(0, m0))
                        self.mm(ov, Ck[k][1].v(pr), pbf[3].v(ksl), start=False, stop=True, tp=(0, m0))
                    yield
                    if pr % 4 != 3:
                        continue
                    yb, y2b, sgq = tP[0], tP[1], wrr[pr % 2]
                    self.stt(yb.v(), uT.v(ft), ds5(ft), py.v(slice(0, NT)), ALU.mult, ALU.add)
                    self.release(by)
                    self.act(y2b.v(), yb.v(), AF.Square)
                    self.ts("dve", y2b.v(), y2b.v(), 0.044715, ALU.mult, 1.0, ALU.add)
                    self.tt("dve", y2b.v(), y2b.v(), yb.v(), ALU.mult)
                    self.act(sgq.v(), y2b.v(), AF.Sigmoid, scale=2.0 * math.sqrt(2.0 / math.pi))
                    self.tt("dve", hS.v(ft), yb.v(), sgq.v(), ALU.mult)
                    yield

            def s5_warm_gen():
                Pr, Pi, P1, P2 = s5a[0], s5a[1], s5a[2], s5a[3]
                sm = small[3].v(slice(32, 64))
                smb = small[3]
                for ft in range(4):
                    bZR = self.bank(hold=True)
                    ZR = self.psf(bZR)
                    bZI = self.bank(hold=True)
                    ZI = self.psf(bZI)
                    for q in range(4):
                        pr = ft * 4 + q
                        r0 = q * 32
                        for k in range(NB):
                            ksl = slice(k * 128, (k + 1) * 128)
                            lh = utok.v(k, slice(pr * 32, (pr + 1) * 32))
                            self.mm(ZR.v(ksl, parts=(r0, r0 + 32)), lh, TT0[0].v(pr), tp=(0, r0))
                            self.mm(ZI.v(ksl, parts=(r0, r0 + 32)), lh, TT0[1].v(pr), tp=(0, r0))
                    yield
                    hbr = HB[0].v(ft).bc(HB[0].ap[:, ft, :, :].rearrange("p a b -> p (a b)"))
                    hbi = HB[1].v(ft).bc(HB[1].ap[:, ft, :, :].rearrange("p a b -> p (a b)"))
                    self.tt("dve", P1.v(), ZR.v(), hbr, ALU.mult)
                    self.tt("dve", P2.v(), ZI.v(), hbi, ALU.mult)
                    self.tt("dve", Pr.v(), P1.v(), P2.v(), ALU.subtract)
                    self.tt("dve", P1.v(), ZR.v(), hbi, ALU.mult)
                    self.tt("dve", P2.v(), ZI.v(), hbr, ALU.mult)
                    self.tt("dve", Pi.v(), P1.v(), P2.v(), ALU.add)
                    self.release(bZR)
                    self.release(bZI)
                    yield
                    bS = self.bank()
                    pS = self.psf(bS)
                    for k in range(NB):
                        ksl = slice(k * 128, (k + 1) * 128)
                        self.mm(pS.v(slice(0, 4)), Pr.v(ksl), Sel.v(), start=(k == 0), stop=(k == NB - 1))
                    for k in range(NB):
                        ksl = slice(k * 128, (k + 1) * 128)
                        self.mm(pS.v(slice(8, 12)), Pi.v(ksl), Sel.v(), start=(k == 0), stop=(k == NB - 1))
                    psl = slice(ft * 4, ft * 4 + 4)
                    cr_v = carry.v(psl).bc(carry.ap[:, ft * 4:ft * 4 + 4, 0])
                    ci_v = carry.v(psl).bc(carry.ap[:, ft * 4:ft * 4 + 4, 1])
                    r5 = R512.v(psl)
                    gc4 = FG.v(4, 0, psl)
                    gs4 = FG.v(4, 1, psl)
                    w_r = smb.v(slice(32, 36))
                    w_i = smb.v(slice(36, 40))
                    ta_ = smb.v(slice(40, 44))
                    tb2 = smb.v(slice(44, 48))
                    self.tt("dve", ta_, cr_v, r5, ALU.mult)
                    self.tt("dve", w_r, ta_, pS.v(slice(0, 4)), ALU.add)
                    self.tt("dve", tb2, ci_v, r5, ALU.mult)
                    self.tt("dve", w_i, tb2, pS.v(slice(8, 12)), ALU.add)
                    self.tt("dve", ta_, w_r, gc4, ALU.mult)
                    self.tt("dve", tb2, w_i, gs4, ALU.mult)
                    self.tt("dve", cr_v, ta_, tb2, ALU.subtract)
                    self.tt("dve", ta_, w_r, gs4, ALU.mult)
                    self.tt("dve", tb2, w_i, gc4, ALU.mult)
                    self.tt("dve", ci_v, ta_, tb2, ALU.add)
                    yield

            g1_ = gla_rest()
            g2_ = s5_warm_gen() if warm else s5_gen()
            alive1 = alive2 = True
            self.ck(8 if warm else 27)
            while alive1 or alive2:
                if alive1:
                    try:
                        next(g1_)
                    except StopIteration:
                        alive1 = False
                if alive2:
                    try:
                        next(g2_)
                    except StopIteration:
                        alive2 = False
            sgb = pbf[0]
            if warm:
                self.act(ctmp.v(0, slice(0, 1)), carry.v(15, slice(1, 2)), AF.Identity, extra_writes=[("wdone", t0)])
                return
            self.ck(28)
            A = Bump(O_A, ASZ)
            xt2 = sb(A(NB * D * 4), (NB, D), F32)
            xin2 = xt2.v()
            self.dma("sp", xin2.ap, src_d[t0:t0 + NT, :].rearrange("(b p) d -> p b d", p=128), reads=[], writes=[xin2])
            wg = self.wload("w_glu", 0, 4, 0, 512)
            for fo in range(4):
                b = self.bank()
                pg = self.psf(b)
                for kc in range(4):
                    self.mm(pg.v(slice(0, NT)), wg.v(kc, slice(fo * 128, (fo + 1) * 128)), hS.v(kc), start=(kc == 0), stop=(kc == 3))
                self.act(sgb.v(), pg.v(slice(0, NT)), AF.Sigmoid, bias=bglu(fo))
                self.tt("dve", oBT.v(fo), hS.v(fo), sgb.v(), ALU.mult)
            if self.dbg and first_main:
                self.dump("oBT", oBT.v(), (4, NT), BF16)

            self.ck(29)
            B = Bump(B_REST, O_B + BSZ - B_REST)
            hS_keep = B(4 * NT * 2)
            sa = [sb(B(NT * 2), (NT,), BF16) for _ in range(2)]
            sbg = [sb(B(NT * 2), (NT,), BF16) for _ in range(2)]
            m1 = [sb(B(NT * 4), (NT,), F32) for _ in range(2)]
            m2 = [sb(B(NT * 4), (NT,), F32) for _ in range(2)]
            for q in range(4):
                wga = self.wload("w_in", 0, 16, OFF_GA + q * 512, 512)
                wgb = self.wload("w_in", 0, 16, OFF_GB + q * 512, 512)
                sl = self.slot_rr
                self.slot_rr = (sl + 1) % 3
                wba = self.sb(self.O_W + sl * self.SLOT, (8, 512), BF16)
                wbb = self.sb(self.O_W + sl * self.SLOT + 8192, (4, 512), BF16)
                wsrc, wpl = self.wb["w_ba"]
                self.dma("sp", wba.v().ap, wsrc[:, q * 512:(q + 1) * 512].rearrange("(kc p) n -> p kc n", p=128),
                         reads=[k for (_, _, _, _, k) in wpl], writes=[wba.v()])
                wsrc, wpl = self.wb["w_bb"]
                self.dma("sp", wbb.v().ap, wsrc[:, q * 512:(q + 1) * 512].rearrange("(kc p) n -> p kc n", p=128),
                         reads=[k for (_, _, _, _, k) in wpl], writes=[wbb.v()])
                for jj in range(4):
                    j = q * 4 + jj
                    bA = self.bank()
                    pgA = self.psf(bA)
                    for kc in range(16):
                        self.mm(pgA.v(slice(0, NT)), wga.v(kc, slice(jj * 128, (jj + 1) * 128)), hT.v(kc), start=(kc == 0), stop=(kc == 15))
                    bB = self.bank()
                    pgB = self.psf(bB)
                    for kc in range(16):
                        self.mm(pgB.v(slice(0, NT)), wgb.v(kc, slice(jj * 128, (jj + 1) * 128)), hT.v(kc), start=(kc == 0), stop=(kc == 15))
                    bPA = self.bank()
                    pPA = self.psf(bPA)
                    for kc in range(8):
                        self.mm(pPA.v(slice(0, NT)), wba.v(kc, slice(jj * 128, (jj + 1) * 128)), oAT.v(kc), start=(kc == 0), stop=(kc == 7))
                    bPB = self.bank()
                    pPB = self.psf(bPB)
                    for kc in range(4):
                        self.mm(pPB.v(slice(0, NT)), wbb.v(kc, slice(jj * 128, (jj + 1) * 128)), oBT.v(kc), start=(kc == 0), stop=(kc == 3))
                    i2 = j % 2
                    self.act(sa[i2].v(), pgA.v(slice(0, NT)), AF.Sigmoid)
                    self.act(sbg[i2].v(), pgB.v(slice(0, NT)), AF.Sigmoid)
                    self.tt("dve", m1[i2].v(), pPA.v(slice(0, NT)), sa[i2].v(), ALU.mult)
                    self.tt("dve", m2[i2].v(), pPB.v(slice(0, NT)), sbg[i2].v(), ALU.mult)
                    self.tt("pool", mixT.v(j), m1[i2].v(), m2[i2].v(), ALU.add)
            if self.dbg and first_main:
                self.dump("mixT", mixT.v(), (16, NT), BF16)

            self.ck(30)
            for cb in range(4):
                wo = self.wload("w_out", 0, 16, cb * 512, 512)
                for blk in range(NB):
                    b = self.bank()
                    po = self.psf(b)
                    for kc in range(16):
                        self.mm(po.v(), mixT.v(kc, slice(blk * 128, (blk + 1) * 128)), wo.v(kc), start=(kc == 0), stop=(kc == 15))
                    xv = xt2.v(blk, slice(cb * 512, (cb + 1) * 512))
                    self.tt("dve", xv, xv, po.v(), ALU.add)
            if self.dbg and first_main:
                self.dump("x1", xt2.v(), (NB, D), F32)
            self.ck(31)
            B = Bump(O_B, BSZ)
            hid = sb(B(44 * NT * 2), (44, NT), BF16)
            sg2 = [sb(B(NT * 4), (NT,), F32) for _ in range(2)]
            xn2 = [sb(O_B + i * D * 2, (D,), BF16) for i in range(2)]
            norm_to_hT(g2c, CP_G2, hT, xt2, xn2)
            self.ck(32)
            for q in range(11):
                wg_ = self.wload("w_ffi", 0, 16, q * 512, 512)
                wu_ = self.wload("w_ffi", 0, 16, DFF + q * 512, 512)
                for jj in range(4):
                    c = q * 4 + jj
                    bG = self.bank()
                    pG = self.psf(bG)
                    for kc in range(16):
                        self.mm(pG.v(slice(0, NT)), wg_.v(kc, slice(jj * 128, (jj + 1) * 128)), hT.v(kc), start=(kc == 0), stop=(kc == 15))
                    bU = self.bank()
                    pU = self.psf(bU)
                    for kc in range(16):
                        self.mm(pU.v(slice(0, NT)), wu_.v(kc, slice(jj * 128, (jj + 1) * 128)), hT.v(kc), start=(kc == 0), stop=(kc == 15))
                    self.act(sg2[c % 2].v(), pG.v(slice(0, NT)), AF.Silu)
                    self.tt("dve", hid.v(c), pU.v(slice(0, NT)), sg2[c % 2].v(), ALU.mult)
            self.ck(33)
            def ffo_gen():
                for cb in range(4):
                    banks = [self.bank(hold=True) for _ in range(NB)]
                    for q in range(4):
                        wo_ = self.wload("w_ffo", q * 11, 11, cb * 512, 512)
                        for blk in range(NB):
                            po = self.psf(banks[blk])
                            for kk in range(11):
                                kc = q * 11 + kk
                                self.mm(po.v(), hid.v(kc, slice(blk * 128, (blk + 1) * 128)), wo_.v(kk),
                                        start=(kc == 0), stop=(kc == 43))
                        yield
                    for blk in range(NB):
                        po = self.psf(banks[blk])
                        xv = xt2.v(blk, slice(cb * 512, (cb + 1) * 512))
                        self.tt("dve", xv, xv, po.v(), ALU.add)
                        self.release(banks[blk])
            ga_ = ffo_gen()
            gb_ = prep_gen(nxt[0], nxt[1], STG_MAIN[0], STG_MAIN[1]) if nxt is not None else iter(())
            la = lb = True
            while la or lb:
                if la:
                    try:
                        next(ga_)
                    except StopIteration:
                        la = False
                if lb:
                    try:
                        next(gb_)
                    except StopIteration:
                        lb = False
            self.ck(34)
            ss = small[0]
            lnv = small[1]
            rstd = small[2]
            for blk in range(NB):
                self.act(xn2[blk % 2].v(), xt2.v(blk), AF.Square, accum=ss.v(slice(blk, blk + 1)))
            self.act(lnv.v(slice(0, NB)), ss.v(slice(0, NB)), AF.Ln, scale=1.0 / D, bias=1e-6)
            self.act(rstd.v(slice(0, NB)), lnv.v(slice(0, NB)), AF.Exp, scale=-0.5)
            gslot = self.wslot(16, 512)
            gfv = View(gslot.ap.rearrange("p a b -> p (a b)").bitcast(F32)[:, 0:D], gslot.v().keys)
            self.dma("sp", gfv.ap, gfb_d[:, :], reads=[], writes=[gfv])
            for blk in range(NB):
                self.stt(xt2.v(blk), xt2.v(blk), rstd.v(slice(blk, blk + 1)), gfv, ALU.mult, ALU.mult)
            key = ("out", t0)
            self.dma("pool", out_d[t0:t0 + NT, :].rearrange("(b p) d -> p b d", p=128), xt2.v().ap, reads=[xt2.v()], writes=[key])
            self.final_reads.append(key)

        try:
            self.ck(1)
            tiles = [(xp_d, SEQH - (self.n_warm - i) * NT, True) for i in range(self.n_warm)]
            tiles += [(x_d, i * NT, False) for i in range(self.n_main)]
            if tiles:
                stg = STG_WARM if tiles[0][2] else STG_MAIN
                for _ in prep_gen(tiles[0][0], tiles[0][1], stg[0], stg[1]):
                    pass
            seen_main = False
            for i, (sd, t0, wm) in enumerate(tiles):
                nxt = (tiles[i + 1][0], tiles[i + 1][1]) if i + 1 < len(tiles) else None
                if not wm and not seen_main:
                    emit_casts(10 ** 6, gate=(("wdone", tiles[i - 1][1]) if i > 0 else None))
                    self.ck(20)
                tile(sd, t0, wm, (not wm) and (not seen_main), nxt)
                if wm:
                    emit_casts(1, gate=("wdone", t0))
                else:
                    seen_main = True
            emit_casts(10 ** 6)
        except StopBuild:
            pass
        self.P.op("sp", lambda h: h.nop(), reads=list(self.final_reads))

        P.analyze()
        sems = {"pe": sem_ctx["s_pe"], "act": sem_ctx["s_act"], "dve": sem_ctx["s_dve"],
                "pool": sem_ctx["s_pool"], "sp": sem_ctx["s_sp"]}
        dma_sems = {"sp": [sem_ctx["dsp%d" % i] for i in range(6)], "pool": [sem_ctx["dpl%d" % i] for i in range(4)],
                    "cast": [sem_ctx["dca%d" % i] for i in range(2)]}
        P.plan(sems, dma_sems)
        blk_ctx = nc.Block()
        block = blk_ctx.__enter__()

        @block.tensor
        def _(e):
            P.emit_engine("pe", e)

        @block.scalar
        def _(e):
            P.emit_engine("act", e)

        @block.vector
        def _(e):
            P.emit_engine("dve", e)

        @block.gpsimd
        def _(e):
            P.emit_engine("pool", e)

        @block.sync
        def _(e):
            P.emit_engine("sp", e)
        blk_ctx.__exit__(None, None, None)
        for c in reversed(ctx):
            c.__exit__(None, None, None)
        return nc


def build_program(n_warm=SEQH // NT, n_main=SEQH // NT, dbg=False, stop_at=10 ** 9):
    nc = bass.Bass("TRN2", target_bir_lowering=False)
    bld = Builder(nc, n_warm, n_main, dbg)
    bld.stop_at = stop_at
    bld.build()
    return nc, bld


def host_consts(inp):
    f = np.float32
    cstp = np.zeros((128, CP_N), f)
    cm = np.ones(NT, f)
    cm[0::128] = 0.0
    cstp[:, CP_CMASK:CP_CMASK + NT] = cm[None, :]
    cstp[:, CP_G1:CP_G1 + 16] = np.asarray(inp["norm1_g"], f).reshape(16, 128).T
    cstp[:, CP_G2:CP_G2 + 16] = np.asarray(inp["norm2_g"], f).reshape(16, 128).T
    cstp[:, CP_GHN:CP_GHN + 8] = np.asarray(inp["gla_norm_g"], f).reshape(8, 128).T
    cstp[:, CP_BA2:CP_BA2 + 4] = np.asarray(inp["b_a2"], f).reshape(4, 128).T
    cstp[:, CP_DS5:CP_DS5 + 4] = np.asarray(inp["s5_d"], f).reshape(4, 128).T
    cstp[:, CP_BGLU:CP_BGLU + 4] = np.asarray(inp["b_glu"], f).reshape(4, 128).T
    cstt = np.zeros((128, CT_N), f)
    cstt[:, CT_ID:CT_ID + 128] = np.eye(128, dtype=f)
    cstt[:, CT_MASK:CT_MASK + 128] = np.triu(np.ones((128, 128), f))
    cstt[0:16, CT_WA2:CT_WA2 + 512] = np.asarray(inp["w_a2"], f).reshape(16, 512)
    lre = np.asarray(inp["lam_re"], f).reshape(16, 2, 64)
    lim = np.asarray(inp["lam_im"], f).reshape(16, 2, 64)
    ldt = np.asarray(inp["log_dt"], f).reshape(16, 2)
    cstt[:, CT_LRE:CT_LRE + 16] = lre.transpose(1, 2, 0).reshape(128, 16)
    cstt[:, CT_LIM:CT_LIM + 16] = lim.transpose(1, 2, 0).reshape(128, 16)
    cstt[:, CT_LDT:CT_LDT + 16] = np.broadcast_to(ldt.T[:, None, :], (2, 64, 16)).reshape(128, 16)
    bre = np.asarray(inp["s5_b_re"], f).reshape(32, 64, 16)
    bim = np.asarray(inp["s5_b_im"], f).reshape(32, 64, 16)
    cre = np.asarray(inp["s5_c_re"], f).reshape(32, 16, 64)
    cim = np.asarray(inp["s5_c_im"], f).reshape(32, 16, 64)
    Bl_re = np.zeros((128, 16, 16), f)
    Bl_im = np.zeros((128, 16, 16), f)
    Cl_re = np.zeros((128, 16, 32), f)
    Cl_im = np.zeros((128, 16, 32), f)
    for pr in range(16):
        ft = pr // 4
        r0 = (pr % 4) * 32
        for g2 in range(2):
            g = 2 * pr + g2
            Bl_re[g2 * 64:(g2 + 1) * 64, pr, :] = bre[g]
            Bl_im[g2 * 64:(g2 + 1) * 64, pr, :] = bim[g]
            Cl_re[g2 * 64:(g2 + 1) * 64, pr, g2 * 16:(g2 + 1) * 16] = cre[g].T
            Cl_im[g2 * 64:(g2 + 1) * 64, pr, g2 * 16:(g2 + 1) * 16] = cim[g].T
    cstt[:, CT_BRE:CT_BRE + 256] = Bl_re.reshape(128, 256)
    cstt[:, CT_BIM:CT_BIM + 256] = Bl_im.reshape(128, 256)
    cstt[:, CT_CRE:CT_CRE + 512] = Cl_re.reshape(128, 512)
    cstt[:, CT_CIM:CT_CIM + 512] = Cl_im.reshape(128, 512)
    return cstp, cstt


def make_in_maps(inp):
    f = np.float32
    x = np.asarray(inp["x"], f)
    cstp, cstt = host_consts(inp)
    shared = {
        "cstp": cstp, "cstt": cstt,
        "gfb": np.ascontiguousarray(np.broadcast_to(np.asarray(inp["final_norm_g"], f).reshape(1, D), (128, D))),
        "w_in": np.ascontiguousarray(np.asarray(inp["w_in"], f).reshape(D, DIN)),
        "w_ba": np.ascontiguousarray(np.asarray(inp["w_branch_a"], f).reshape(1024, D)),
        "w_bb": np.ascontiguousarray(np.asarray(inp["w_branch_b"], f).reshape(512, D)),
        "w_out": np.ascontiguousarray(np.asarray(inp["w_out"], f).reshape(D, D)),
        "w_glu": np.ascontiguousarray(np.asarray(inp["w_glu"], f).reshape(512, 512)),
        "w_ffi": np.ascontiguousarray(np.asarray(inp["w_ffn_in"], f).reshape(D, 2 * DFF)),
        "w_ffo": np.ascontiguousarray(np.asarray(inp["w_ffn_out"], f).reshape(DFF, D)),
    }
    zeros = np.zeros((SEQH, D), f)
    in_maps = []
    for core in range(8):
        b, half = core // 2, core % 2
        m = dict(shared)
        m["x"] = np.ascontiguousarray(x[b, half * SEQH:(half + 1) * SEQH])
        m["xp"] = np.ascontiguousarray(x[b, 0:SEQH]) if half == 1 else zeros
        in_maps.append(m)
    return in_maps


_CACHE = {}


def kernel(**inputs):
    in_maps = make_in_maps(inputs)
    if "nc" not in _CACHE:
        _CACHE["nc"] = build_program()[0]
    nc = _CACHE["nc"]
    res = run_bass_kernel_spmd(nc, in_maps, core_ids=list(range(8)))
    out = np.zeros((4, 2 * SEQH, D), np.float32)
    for core in range(8):
        b, half = core // 2, core % 2
        out[b, half * SEQH:(half + 1) * SEQH] = res.results[core]["out"]
    return out
```

```python
import itertools
import math
import numpy as np
import concourse.bass as bass
import concourse.mybir as mybir
from concourse.bass_utils import run_bass_kernel_spmd

F32 = mybir.dt.float32
BF16 = mybir.dt.bfloat16
I32 = mybir.dt.int32
U8 = mybir.dt.uint8
AF = mybir.ActivationFunctionType
ALU = mybir.AluOpType

D = 2048
DIN = 7696
DFF = 5632
SEQH = 4096
NT = 512
NB = NT // 128
OFF_Q, OFF_K, OFF_V, OFF_R, OFF_AL, OFF_U, OFF_GA, OFF_GB = 0, 512, 1024, 2048, 3072, 3088, 3600, 5648
PAGE = 256
ENG_NAMES = ("pe", "act", "dve", "pool", "sp")
TWO_PI = 2.0 * math.pi
PI_SAFE = 3.1415925

CP_CMASK = 0
CP_G1 = CP_CMASK + NT
CP_G2 = CP_G1 + 16
CP_GHN = CP_G2 + 16
CP_BA2 = CP_GHN + 8
CP_DS5 = CP_BA2 + 4
CP_BGLU = CP_DS5 + 4
CP_N = CP_BGLU + 4
CT_ID = 0
CT_MASK = 128
CT_WA2 = 256
CT_LRE = CT_WA2 + 512
CT_LIM = CT_LRE + 16
CT_LDT = CT_LIM + 16
CT_BRE = CT_LDT + 16
CT_BIM = CT_BRE + 256
CT_CRE = CT_BIM + 256
CT_CIM = CT_CRE + 512
CT_N = CT_CIM + 512


class View:
    __slots__ = ("ap", "keys")

    def __init__(self, ap, keys):
        self.ap = ap
        self.keys = keys

    def bc(self, ap):
        return View(ap, self.keys)


class Buf:
    def __init__(self, space, base, off, shape, dt, es, raw=False):
        self.space = space
        self.off = off
        self.shape = tuple(int(s) for s in shape)
        self.es = es
        n = int(np.prod(self.shape))
        if raw:
            ap = base
        else:
            ap = base[:, off:off + n * es].bitcast(dt)
        if len(self.shape) > 1:
            names = ["d%d" % i for i in range(len(self.shape))]
            kw = {nm: s for nm, s in zip(names[:-1], self.shape[:-1])}
            ap = ap.rearrange("p (%s) -> p %s" % (" ".join(names), " ".join(names)), **kw)
        self.ap = ap
        st = [es] * len(self.shape)
        for i in range(len(self.shape) - 2, -1, -1):
            st[i] = st[i + 1] * self.shape[i + 1]
        self.strides = st

    def v(self, *idx, parts=None):
        shape = self.shape
        idx = list(idx) + [slice(None)] * (len(shape) - len(idx))
        rng = []
        for i, s in zip(idx, shape):
            if isinstance(i, slice):
                a, b, _ = i.indices(s)
                rng.append((a, b))
            else:
                rng.append((i, i + 1))
        L = len(shape) - 1
        while L > 0 and rng[L] == (0, shape[L]):
            L -= 1
        pages = set()
        st = self.strides
        for combo in itertools.product(*[range(a, b) for a, b in rng[:L]]):
            base = self.off
            for c, s_ in zip(combo, st):
                base += c * s_
            lo = base + rng[L][0] * st[L]
            hi = base + rng[L][1] * st[L]
            pages.update(range(lo // PAGE, (hi - 1) // PAGE + 1))
        sp = self.space
        if sp == "p":
            keys = (("p", self.off // 2048),)
        else:
            keys = tuple((sp, p) for p in pages)
        psl = slice(None) if parts is None else slice(parts[0], parts[1])
        return View(self.ap[(psl,) + tuple(idx)], keys)


class Op:
    __slots__ = ("eng", "fn", "rk", "wk", "dma", "idx", "waits", "signal", "count", "dsem", "dval", "wl", "inc", "dq")


class Prog:
    def __init__(self):
        self.ops = []

    def op(self, eng, fn, reads=(), writes=(), dma=False, dq=None):
        o = Op()
        o.eng = eng
        o.dq = dq if dq is not None else eng
        o.fn = fn
        rk = []
        for r in reads:
            if isinstance(r, View):
                rk.extend(r.keys)
            elif r is not None:
                rk.append(r)
        wk = []
        for w in writes:
            if isinstance(w, View):
                wk.extend(w.keys)
            elif w is not None:
                wk.append(w)
        for k in rk:
            if isinstance(k, tuple) and k[0] == "p" and k not in wk:
                wk.append(k)
        o.rk = rk
        o.wk = wk
        o.dma = dma
        o.idx = len(self.ops)
        o.waits = []
        o.signal = False
        o.count = 0
        o.dsem = None
        o.dval = 0
        self.ops.append(o)
        return o

    def analyze(self):
        ops = self.ops
        last_writer = {}
        readers = {}
        for o in ops:
            deps = {}
            for k in o.rk:
                w = last_writer.get(k)
                if w is not None:
                    deps[w] = True
            for k in o.wk:
                w = last_writer.get(k)
                if w is not None and w not in deps:
                    deps[w] = False
                rd = readers.get(k)
                if rd:
                    for r in rd.values():
                        if r not in deps:
                            deps[r] = False
            deps.pop(o.idx, None)
            for d, raw in deps.items():
                p = ops[d]
                if p.dma:
                    o.waits.append(d)
                    continue
                if (not o.dma) and p.eng == o.eng:
                    if o.eng == "pe":
                        continue
                o.waits.append(d)
                p.signal = True
            rkey = ("D", o.idx) if o.dma else o.eng
            for k in o.rk:
                rd = readers.get(k)
                if rd is None:
                    readers[k] = {rkey: o.idx}
                else:
                    rd[rkey] = o.idx
            for k in o.wk:
                last_writer[k] = o.idx
                readers[k] = None

    def plan(self, sems, dma_sems):
        ops = self.ops
        counts = {e: 0 for e in ENG_NAMES}
        for o in ops:
            if (not o.dma) and o.signal:
                counts[o.eng] += 1
                o.count = counts[o.eng]
        rr = {q: 0 for q in dma_sems}
        dvals = {q: [0] * len(dma_sems[q]) for q in dma_sems}
        seen = {e: {} for e in ENG_NAMES}
        self.n_wait = 0
        self.per_eng = {e: [] for e in ENG_NAMES}
        for o in ops:
            e = o.eng
            o.wl = []
            o.inc = None
            sd = seen[e]

            def do_wait(sem, val, key):
                if sd.get(key, 0) >= val:
                    return
                sd[key] = val
                o.wl.append((sem, val))
                self.n_wait += 1

            if o.dma:
                q = o.dq
                i = rr[q]
                rr[q] = (i + 1) % len(dma_sems[q])
                if dvals[q][i] > 0:
                    do_wait(dma_sems[q][i], dvals[q][i], ("d", q, i))
                dvals[q][i] += 16
                o.dsem = (q, i)
                o.dval = dvals[q][i]
                o.inc = (dma_sems[q][i], 16)
            elif o.signal:
                o.inc = (sems[e], 1)
            for d in o.waits:
                p = ops[d]
                if p.dma:
                    pe_, pi = p.dsem
                    do_wait(dma_sems[pe_][pi], p.dval, ("d", pe_, pi))
                else:
                    do_wait(sems[p.eng], p.count, ("e", p.eng))
            self.per_eng[e].append(o)

    def emit_engine(self, e, h):
        for o in self.per_eng[e]:
            for sem, val in o.wl:
                h.wait_ge(sem, val)
            inst = o.fn(h)
            if o.inc is not None:
                inst.then_inc(o.inc[0], o.inc[1])


class StopBuild(Exception):
    pass


class Builder:
    stop_at = 10 ** 9

    def ck(self, n):
        if n >= self.stop_at:
            raise StopBuild()

    def __init__(self, nc, n_warm, n_main, dbg):
        self.nc = nc
        self.P = Prog()
        self.n_warm = n_warm
        self.n_main = n_main
        self.dbg = dbg
        self.dbg_out = {}
        self.bank_rr = 0
        self.bank_held = set()
        self.slot_rr = 0

    def mm(self, out, lhsT, rhs, start=True, stop=True, tp=None):
        kw = {} if tp is None else {"tile_position": tp}
        self.P.op("pe", lambda h: h.matmul(out.ap, lhsT.ap, rhs.ap, start=start, stop=stop, **kw),
                  reads=[lhsT, rhs], writes=[out])

    def tr(self, out, in_):
        ident = self.ident
        self.P.op("pe", lambda h: h.transpose(out.ap, in_.ap, ident.ap), reads=[in_, ident], writes=[out])

    def act(self, out, in_, func, scale=None, bias=None, accum=None, extra_writes=()):
        reads = [in_]
        kw = {}
        if isinstance(scale, View):
            reads.append(scale)
            kw["scale"] = scale.ap
        elif scale is not None:
            kw["scale"] = float(scale)
        if isinstance(bias, View):
            reads.append(bias)
            kw["bias"] = bias.ap
        elif bias is not None:
            kw["bias"] = float(bias)
        writes = [out] + list(extra_writes)
        if accum is not None:
            writes.append(accum)
            kw["accum_out"] = accum.ap
        self.P.op("act", lambda h: h.activation(out=out.ap, in_=in_.ap, func=func, **kw), reads=reads, writes=writes)

    def tt(self, eng, out, a, b, op, extra_writes=()):
        self.P.op(eng, lambda h: h.tensor_tensor(out=out.ap, in0=a.ap, in1=b.ap, op=op), reads=[a, b],
                  writes=[out] + list(extra_writes))

    def ts(self, eng, out, a, s1, op0, s2=None, op1=None):
        reads = [a]
        v1 = s1
        if isinstance(s1, View):
            reads.append(s1)
            v1 = s1.ap
        v2 = s2
        if isinstance(s2, View):
            reads.append(s2)
            v2 = s2.ap
        if op1 is None:
            self.P.op(eng, lambda h: h.tensor_scalar(out=out.ap, in0=a.ap, scalar1=v1, scalar2=None, op0=op0),
                      reads=reads, writes=[out])
        else:
            self.P.op(eng, lambda h: h.tensor_scalar(out=out.ap, in0=a.ap, scalar1=v1, scalar2=v2, op0=op0, op1=op1),
                      reads=reads, writes=[out])

    def stt(self, out, a, s, b, op0, op1):
        reads = [a, b]
        sv = s
        if isinstance(s, View):
            reads.append(s)
            sv = s.ap
        self.P.op("dve", lambda h: h.scalar_tensor_tensor(out=out.ap, in0=a.ap, scalar=sv, in1=b.ap, op0=op0, op1=op1),
                  reads=reads, writes=[out])

    def scan(self, out, d0, d1, init):
        reads = [d0, d1]
        iv = init
        if isinstance(init, View):
            reads.append(init)
            iv = init.ap
        self.P.op("dve", lambda h: h.tensor_tensor_scan(out=out.ap, data0=d0.ap, data1=d1.ap, initial=iv,
                                                       op0=ALU.mult, op1=ALU.add), reads=reads, writes=[out])

    def cp(self, eng, out, in_, extra_writes=()):
        if eng == "act":
            self.act(out, in_, AF.Identity, extra_writes=extra_writes)
        else:
            self.P.op(eng, lambda h: h.tensor_copy(out=out.ap, in_=in_.ap), reads=[in_], writes=[out])

    def memset(self, eng, out, val):
        self.P.op(eng, lambda h: h.memset(out.ap, val), writes=[out])

    def dma(self, q, out_ap, in_ap, reads, writes, dq=None):
        self.P.op(q, lambda h: h.dma_start(out=out_ap, in_=in_ap), reads=reads, writes=writes, dma=True, dq=dq)

    def sb(self, off, shape, dt):
        es = {F32: 4, BF16: 2, I32: 4}[dt]
        n = int(np.prod(shape)) * es
        assert off % PAGE == 0, (off, shape)
        assert off + n <= self.arena_size, (off, n, self.arena_size)
        return Buf("s", self.arena, off, shape, dt, es)

    def bank(self, hold=False):
        for _ in range(8):
            b = self.bank_rr
            self.bank_rr = (b + 1) % 8
            if b not in self.bank_held:
                if hold:
                    self.bank_held.add(b)
                return b
        raise RuntimeError("all PSUM banks held")

    def release(self, b):
        self.bank_held.discard(b)

    def psf(self, b):
        return Buf("p", self.psum[b], b * 2048, (512,), F32, 4, raw=True)

    def psb(self, b):
        return Buf("p", self.psum[b].bitcast(BF16), b * 2048, (1024,), BF16, 2, raw=True)

    def wslot(self, nkc, ncols):
        s = self.slot_rr
        self.slot_rr = (s + 1) % 3
        assert nkc * ncols * 2 <= self.SLOT
        return self.sb(self.O_W + s * self.SLOT, (nkc, ncols), BF16)

    def wload(self, wname, kc0, nkc, c0, ncols):
        w_ap, plist = self.wb[wname]
        r_lo, r_hi = kc0 * 128, (kc0 + nkc) * 128
        keys = [k for (a, b, ra, rb_, k) in plist if a < c0 + ncols and c0 < b and ra < r_hi and r_lo < rb_]
        buf = self.wslot(nkc, ncols)
        src = w_ap[kc0 * 128:(kc0 + nkc) * 128, c0:c0 + ncols].rearrange("(kc p) n -> p kc n", p=128)
        dst = buf.v()
        self.dma("sp", dst.ap, src, reads=keys, writes=[dst])
        return buf

    def dump(self, name, view, shape, dt):
        if not self.dbg:
            return
        t = self.nc.dram_tensor("dbg_" + name, [128] + list(shape), dt, kind="ExternalOutput").ap()
        self.dbg_out[name] = t
        self.dma("pool", t, view.ap, reads=[view], writes=["dbg_" + name])
        self.final_reads.append("dbg_" + name)

    def build(self):
        nc = self.nc
        P = self.P
        self.final_reads = []
        x_d = nc.dram_tensor("x", [SEQH, D], F32, kind="ExternalInput").ap()
        xp_d = nc.dram_tensor("xp", [SEQH, D], F32, kind="ExternalInput").ap()
        out_d = nc.dram_tensor("out", [SEQH, D], F32, kind="ExternalOutput").ap()
        cstp_d = nc.dram_tensor("cstp", [128, CP_N], F32, kind="ExternalInput").ap()
        cstt_d = nc.dram_tensor("cstt", [128, CT_N], F32, kind="ExternalInput").ap()
        gfb_d = nc.dram_tensor("gfb", [128, D], F32, kind="ExternalInput").ap()
        def cols(c0, n, step=512):
            return [(c0 + i, min(step, n - i), None) for i in range(0, n, step)]

        def rows(K, N, step=512):
            return [(0, N, (r, min(step, K - r))) for r in range(0, K, step)]
        win_pieces = ([(OFF_AL, 528, None)] + cols(OFF_K, 512) + cols(OFF_V, 1024) + cols(OFF_Q, 512) + cols(OFF_R, 1024)
                      + cols(OFF_GA, 2048) + cols(OFF_GB, 2048))
        ffi_pieces = []
        for i in range(11):
            ffi_pieces += [(i * 512, 512, None), (DFF + i * 512, 512, None)]
        wspec = [("w_in", D, DIN, win_pieces), ("w_glu", 512, 512, cols(0, 512)), ("w_ba", 1024, D, cols(0, D, 1024)),
                 ("w_bb", 512, D, cols(0, D, 2048)), ("w_out", D, D, cols(0, D)), ("w_ffi", D, 2 * DFF, ffi_pieces),
                 ("w_ffo", DFF, D, rows(DFF, D))]
        self.wb = {}
        pending = []
        for name, K, N, pieces in wspec:
            src = nc.dram_tensor(name, [K, N], F32, kind="ExternalInput").ap()
            dst = nc.dram_tensor(name + "_b", [K, N], BF16, kind="Internal").ap()
            plist = []
            for i, (c0, cw, rr_) in enumerate(pieces):
                key = (name, i)
                if rr_ is None:
                    plist.append((c0, c0 + cw, 0, K, key))
                    pending.append((dst[:, c0:c0 + cw], src[:, c0:c0 + cw], key))
                else:
                    r0_, rn_ = rr_
                    plist.append((c0, c0 + cw, r0_, r0_ + rn_, key))
                    pending.append((dst[r0_:r0_ + rn_, c0:c0 + cw], src[r0_:r0_ + rn_, c0:c0 + cw], key))
            self.wb[name] = (dst, plist)

        def emit_casts(n, gate=None):
            for _ in range(min(n, len(pending))):
                d_, s_, key = pending.pop(0)
                self.dma("pool", d_, s_, reads=([gate] if gate is not None else []), writes=[key], dq="cast")
        emit_casts(4)

        o = 0

        def take(n):
            nonlocal o
            r = o
            o += (n + PAGE - 1) // PAGE * PAGE
            return r
        O_CSTP = take(CP_N * 4)
        O_IDENT = take(256)
        O_MASK = take(256)
        O_WA2 = take(1024)
        O_NBA2 = take(16)
        O_EC = take(16 * 128 * 4)
        O_ES = take(16 * 128 * 4)
        O_RHO = take(64)
        O_FG = take(5 * 3 * 16 * 4)
        O_BK = take(NB * 2 * 1024)
        O_CK = take(NB * 2 * 1024)
        O_CARRY = take(16 * 2 * 4)
        O_STATE = take(4 * 256 * 4)
        O_STATEB = take(4 * 256 * 2)
        O_SMALL = take(2048)
        O_WSM = take(512)
        O_H = take(16 * NT * 2)
        self.SLOT = 16 * 512 * 2
        self.O_W = take(3 * self.SLOT)
        O_MIX = take(16 * NT * 2)
        O_BIG = o
        self.arena_size = 212000 // PAGE * PAGE
        BIGSZ = self.arena_size - O_BIG
        assert BIGSZ >= 80 * 1024, BIGSZ

        ctx = []
        arena_t = nc.sbuf_tensor("arena", [128, self.arena_size], U8)
        self.arena = arena_t.__enter__()
        ctx.append(arena_t)
        self.psum = []
        for b in range(8):
            t = nc.psum_tensor("ps%d" % b, [128, 512], F32)
            self.psum.append(t.__enter__()[:])
            ctx.append(t)
        sem_ctx = {}
        for nm in ["s_pe", "s_act", "s_dve", "s_pool", "s_sp"] + ["dsp%d" % i for i in range(6)] + ["dpl%d" % i for i in range(4)] + ["dca%d" % i for i in range(2)]:
            c = nc.semaphore(nm)
            sem_ctx[nm] = c.__enter__()
            ctx.append(c)
        self.arena = self.arena[:]

        sb = self.sb
        cstp = sb(O_CSTP, (CP_N,), F32)
        self.ident = None
        ident_b = sb(O_IDENT, (128,), BF16)
        mask_b = sb(O_MASK, (128,), BF16)
        wa2_b = sb(O_WA2, (512,), BF16)
        nba2 = sb(O_NBA2, (4,), F32)
        Ec = sb(O_EC, (16, 128), F32)
        Es = sb(O_ES, (16, 128), F32)
        rho = sb(O_RHO, (16,), F32)
        FG = sb(O_FG, (5, 3, 16), F32)
        Bk = [[sb(O_BK + (k * 2 + c) * 1024, (4, 128), BF16) for c in range(2)] for k in range(NB)]
        Ck = [[sb(O_CK + (k * 2 + c) * 1024, (16, 32), BF16) for c in range(2)] for k in range(NB)]
        carry = sb(O_CARRY, (16, 2), F32)
        state = sb(O_STATE, (4, 256), F32)
        state_b = sb(O_STATEB, (4, 256), BF16)
        small = [sb(O_SMALL + i * 256, (64,), F32) for i in range(8)]
        R512 = sb(O_WSM, (16,), F32)
        Sel = sb(O_WSM + 256, (4,), F32)
        hT = sb(O_H, (16, NT), BF16)
        mixT = sb(O_MIX, (16, NT), BF16)
        self.ident = ident_b.v()

        def v3(buf, a=NB):
            v = buf.v()
            return v.bc(buf.ap.rearrange("p (a b) -> p a b", a=a))

        def ps3(pbuf, n, a):
            v = pbuf.v(slice(0, n))
            return v.bc(pbuf.ap[:, 0:n].rearrange("p (a b) -> p a b", a=a))


        cmask = cstp.v(slice(CP_CMASK, CP_CMASK + NT))
        g1c = cstp.v(slice(CP_G1, CP_G1 + 16))
        g2c = cstp.v(slice(CP_G2, CP_G2 + 16))
        ghn = cstp.v(slice(CP_GHN, CP_GHN + 8))
        ds5 = lambda ft: cstp.v(slice(CP_DS5 + ft, CP_DS5 + ft + 1))
        bglu = lambda ft: cstp.v(slice(CP_BGLU + ft, CP_BGLU + ft + 1))

        self.dma("sp", cstp.v().ap, cstp_d[:, :], reads=[], writes=[cstp.v()])
        ctt = sb(O_BIG, (CT_N,), F32)
        self.dma("sp", ctt.v().ap, cstt_d[:, :], reads=[], writes=[ctt.v()])
        tmpb = O_BIG + (CT_N * 4 + PAGE - 1) // PAGE * PAGE
        self.cp("dve", ident_b.v(), ctt.v(slice(CT_ID, CT_ID + 128)))
        self.cp("dve", mask_b.v(), ctt.v(slice(CT_MASK, CT_MASK + 128)))
        self.cp("dve", wa2_b.v(), ctt.v(slice(CT_WA2, CT_WA2 + 512)))
        self.ts("dve", nba2.v(), cstp.v(slice(CP_BA2, CP_BA2 + 4)), -1.0, ALU.mult)
        self.memset("dve", state.v(), 0.0)
        self.memset("dve", state_b.v(), 0.0)
        self.memset("dve", carry.v(), 0.0)

        try:
            self.ck(0)
        except StopBuild:
            self.stop_at = -1
        def t16(i):
            return sb(tmpb + i * 256, (16,), F32)
        lre = ctt.v(slice(CT_LRE, CT_LRE + 16))
        lim = ctt.v(slice(CT_LIM, CT_LIM + 16))
        ldt = ctt.v(slice(CT_LDT, CT_LDT + 16))
        dt_ = t16(0).v()
        self.act(dt_, ldt, AF.Exp)
        yv = t16(1).v()
        self.tt("dve", yv, lre, dt_, ALU.mult)
        pv = t16(2).v()
        self.ts("dve", pv, yv, 1.0 / 720.0, ALU.mult, 1.0 / 120.0, ALU.add)
        for cst in (1.0 / 24.0, 1.0 / 6.0, 0.5, 1.0, 1.0):
            self.tt("dve", pv, pv, yv, ALU.mult)
            self.ts("dve", pv, pv, cst, ALU.add)
        self.cp("dve", rho.v(), pv)
        th = t16(3).v()
        self.tt("dve", th, lim, dt_, ALU.mult)

        def sin_of(theta, slot, shift):
            a = t16(slot).v()
            if shift != 0.0:
                self.ts("dve", a, theta, shift, ALU.add)
            else:
                self.cp("dve", a, theta)
            t = t16(slot + 1).v()
            self.ts("dve", t, a, 1.0 / TWO_PI, ALU.mult)
            ki = sb(tmpb + (slot + 2) * 256, (16,), I32).v()
            self.cp("dve", ki, t)
            kf = t16(slot + 3).v()
            self.cp("dve", kf, ki)
            r = t16(slot + 1).v()
            self.stt(r, kf, -TWO_PI, a, ALU.mult, ALU.add)
            self.ts("dve", r, r, -PI_SAFE, ALU.max, PI_SAFE, ALU.min)
            s_ = t16(slot + 3).v()
            self.act(s_, r, AF.Sin)
            return s_
        sinv = sin_of(th, 4, 0.0)
        cosv = sin_of(th, 8, math.pi / 2.0)
        self.cp("dve", Ec.v(slice(None), slice(0, 1)), cosv.bc(cosv.ap.unsqueeze(2)))
        self.cp("dve", Es.v(slice(None), slice(0, 1)), sinv.bc(sinv.ap.unsqueeze(2)))
        tA = sb(tmpb + 16 * 256, (16, 64), F32)
        tB = sb(tmpb + 16 * 256 + 4096, (16, 64), F32)
        n = 1
        while n < 128:
            src_c = Ec.v(slice(None), slice(0, n))
            src_s = Es.v(slice(None), slice(0, n))
            mc_v = Ec.v(slice(None), slice(n - 1, n))
            ms_v = Es.v(slice(None), slice(n - 1, n))
            mc = mc_v.bc(mc_v.ap.broadcast_to([128, 16, n]))
            ms = ms_v.bc(ms_v.ap.broadcast_to([128, 16, n]))
            a_ = tA.v(slice(None), slice(0, n))
            b_ = tB.v(slice(None), slice(0, n))
            self.tt("dve", a_, src_c, mc, ALU.mult)
            self.tt("dve", b_, src_s, ms, ALU.mult)
            self.tt("dve", Ec.v(slice(None), slice(n, 2 * n)), a_, b_, ALU.subtract)
            a2 = tA.v(slice(None), slice(0, n))
            b2 = tB.v(slice(None), slice(0, n))
            self.tt("dve", a2, src_c, ms, ALU.mult)
            self.tt("dve", b2, src_s, mc, ALU.mult)
            self.tt("dve", Es.v(slice(None), slice(n, 2 * n)), a2, b2, ALU.add)
            n *= 2
        rc = t16(12).v()
        rs = t16(13).v()
        self.tt("dve", rc, rho.v(), cosv, ALU.mult)
        self.ts("dve", rc, rc, -1.0, ALU.add)
        self.tt("dve", rs, rho.v(), sinv, ALU.mult)
        den = t16(14).v()
        t2 = t16(15).v()
        self.tt("dve", den, lre, lre, ALU.mult)
        self.tt("dve", t2, lim, lim, ALU.mult)
        self.tt("dve", den, den, t2, ALU.add)
        self.P.op("dve", lambda h: h.reciprocal(out=den.ap, in_=den.ap), reads=[den], writes=[den])
        fre = t16(0).v()
        fim = t16(1).v()
        self.tt("dve", fre, rc, lre, ALU.mult)
        self.tt("dve", t2, rs, lim, ALU.mult)
        self.tt("dve", fre, fre, t2, ALU.add)
        self.tt("dve", fre, fre, den, ALU.mult)
        self.tt("dve", fim, rs, lre, ALU.mult)
        self.tt("dve", t2, rc, lim, ALU.mult)
        self.tt("dve", fim, fim, t2, ALU.subtract)
        self.tt("dve", fim, fim, den, ALU.mult)
        ta = t16(2).v()
        tb_ = t16(3).v()

        def fc(k):
            return FG.v(k, 0)

        def fs(k):
            return FG.v(k, 1)
        self.memset("dve", FG.v(), 0.0)
        self.memset("dve", fc(0), 1.0)
        e127c = Ec.v(slice(None), slice(127, 128))
        e127s = Es.v(slice(None), slice(127, 128))
        self.cp("dve", fc(1), e127c.bc(Ec.ap[:, :, 127]))
        self.cp("dve", fs(1), e127s.bc(Es.ap[:, :, 127]))
        for k in range(2, 5):
            self.tt("dve", ta, fc(k - 1), fc(1), ALU.mult)
            self.tt("dve", tb_, fs(k - 1), fs(1), ALU.mult)
            self.tt("dve", fc(k), ta, tb_, ALU.subtract)
            self.tt("dve", ta, fc(k - 1), fs(1), ALU.mult)
            self.tt("dve", tb_, fs(k - 1), fc(1), ALU.mult)
            self.tt("dve", fs(k), ta, tb_, ALU.add)
        self.ts("dve", FG.v(4, 2), fs(4), -1.0, ALU.mult)
        big0 = tmpb + 16 * 256 + 8192
        cre = ctt.v(slice(CT_CRE, CT_CRE + 512)).bc(ctt.ap[:, CT_CRE:CT_CRE + 512].rearrange("p (a b) -> p a b", a=16))
        cim = ctt.v(slice(CT_CIM, CT_CIM + 512)).bc(ctt.ap[:, CT_CIM:CT_CIM + 512].rearrange("p (a b) -> p a b", a=16))
        bpr = ctt.v(slice(CT_BRE, CT_BRE + 256)).bc(ctt.ap[:, CT_BRE:CT_BRE + 256].rearrange("p (a b) -> p a b", a=16))
        bpi = ctt.v(slice(CT_BIM, CT_BIM + 256)).bc(ctt.ap[:, CT_BIM:CT_BIM + 256].rearrange("p (a b) -> p a b", a=16))

        def b32(v):
            return v.bc(v.ap.unsqueeze(2).broadcast_to([128, 16, 32]))

        def b16(v):
            return v.bc(v.ap.unsqueeze(2).broadcast_to([128, 16, 16]))
        tC = sb(big0, (16, 32), F32)
        tD = sb(big0 + 2048, (16, 32), F32)
        cfr = sb(big0 + 4096, (16, 32), F32)
        cfi = sb(big0 + 6144, (16, 32), F32)
        tE = sb(big0 + 8192, (16, 16), F32)
        tF = sb(big0 + 9216, (16, 16), F32)
        tG = sb(big0 + 10240, (16, 16), F32)
        Mm = sb(big0 + 11264, (16, 32), BF16)
        self.tt("dve", tC.v(), cre, b32(fre), ALU.mult)
        self.tt("dve", tD.v(), cim, b32(fim), ALU.mult)
        self.tt("dve", cfr.v(), tC.v(), tD.v(), ALU.subtract)
        self.tt("dve", tC.v(), cre, b32(fim), ALU.mult)
        self.tt("dve", tD.v(), cim, b32(fre), ALU.mult)
        self.tt("dve", cfi.v(), tC.v(), tD.v(), ALU.add)
        def build_B(mc_, ms_, c, dstv, three_d=False):
            if c == 0:
                self.tt("dve", tE.v(), bpr, b16(mc_), ALU.mult)
                self.tt("dve", tF.v(), bpi, b16(ms_), ALU.mult)
                self.tt("dve", tG.v(), tE.v(), tF.v(), ALU.add)
            else:
                self.tt("dve", tE.v(), bpi, b16(mc_), ALU.mult)
                self.tt("dve", tF.v(), bpr, b16(ms_), ALU.mult)
                self.tt("dve", tG.v(), tE.v(), tF.v(), ALU.subtract)
            self.memset("dve", Mm.v(), 0.0)
            self.cp("dve", Mm.v(slice(None), slice(0, 16), parts=(0, 64)), tG.v(parts=(0, 64)))
            self.cp("dve", Mm.v(slice(None), slice(16, 32), parts=(64, 128)), tG.v(parts=(64, 128)))
            b = self.bank()
            pb = self.psb(b)
            for ft in range(4):
                src = Mm.v(slice(ft * 4, ft * 4 + 4))
                self.tr(pb.v(slice(ft * 128, (ft + 1) * 128)),
                        src.bc(Mm.ap[:, ft * 4:ft * 4 + 4, :].rearrange("p a b -> p (a b)")))
            if three_d:
                self.cp("dve", dstv, ps3(pb, 512, 4))
            else:
                self.cp("dve", dstv, pb.v(slice(0, 512)))

        for k in range(NB):
            self.tt("dve", tC.v(), cfr.v(), b32(fc(k)), ALU.mult)
            self.tt("dve", tD.v(), cfi.v(), b32(fs(k)), ALU.mult)
            self.tt("dve", Ck[k][0].v(), tC.v(), tD.v(), ALU.subtract)
            self.tt("dve", tC.v(), cfr.v(), b32(fs(k)), ALU.mult)
            self.tt("dve", tD.v(), cfi.v(), b32(fc(k)), ALU.mult)
            self.tt("dve", Ck[k][1].v(), tC.v(), tD.v(), ALU.add)
            for c in range(2):
                dstv = Bk[k][c].v().bc(Bk[k][c].ap.rearrange("p a b -> p (a b)"))
                build_B(fc(k), fs(k), c, dstv)

        O_B_ = O_BIG + NB * D * 4
        W_ALOW, W_UTOK, W_VT, W_STGX, W_STGN = O_B_, O_B_ + 1024, O_B_ + 5120, O_B_ + 13312, O_B_ + 21504
        W_TT0 = [O_B_ + 29696, O_B_ + 33792]
        W_HB = [O_B_ + 37888, O_B_ + 41984]
        TT0 = [sb(W_TT0[c], (16, 128), BF16) for c in range(2)]
        HB = [sb(W_HB[c], (4, NB, 128), BF16) for c in range(2)]
        if self.n_warm > 0:
            rp = [rho.v()]
            pw = sb(big0 + 12288, (8, 16), F32)
            for i in range(7):
                self.tt("dve", pw.v(i), rp[-1], rp[-1], ALU.mult)
                rp.append(pw.v(i))
            r128 = rp[7]
            r256 = pw.v(7)
            self.tt("dve", r256, r128, r128, ALU.mult)
            r384 = t16(4).v()
            self.tt("dve", r384, r256, r128, ALU.mult)
            self.tt("dve", R512.v(), r256, r256, ALU.mult)
            Dt = sb(big0 + 13312, (16, 128), F32)
            self.memset("dve", Dt.v(slice(None), slice(127, 128)), 1.0)
            n = 1
            i = 0
            while n < 128:
                srcv = Dt.v(slice(None), slice(128 - n, 128))
                mv = rp[i]
                mb = mv.bc(mv.ap.unsqueeze(2).broadcast_to([128, 16, n]))
                self.tt("dve", Dt.v(slice(None), slice(128 - 2 * n, 128 - n)), srcv, mb, ALU.mult)
                n *= 2
                i += 1
            T0b = sb(big0 + 21504, (16, 128), BF16)
            for c in range(2):
                if c == 0:
                    self.tt("dve", T0b.v(), Dt.v(), Ec.v(), ALU.mult)
                else:
                    self.stt(T0b.v(), Dt.v(), -1.0, Es.v(), ALU.mult, ALU.mult)
                for g in range(4):
                    b = self.bank()
                    pb = self.psb(b)
                    for j in range(4):
                        pr = g * 4 + j
                        self.tr(pb.v(slice(j * 128, (j + 1) * 128)), T0b.v(pr))
                    self.cp("dve", TT0[c].v(slice(g * 4, g * 4 + 4)), ps3(pb, 512, 4))
            hcs = [t16(5).v(), t16(6).v()]
            for k in range(NB):
                rk = [r384, r256, r128, None][k]
                if rk is None:
                    mc_, ms_ = fc(k), fs(k)
                else:
                    self.tt("dve", hcs[0], fc(k), rk, ALU.mult)
                    self.tt("dve", hcs[1], fs(k), rk, ALU.mult)
                    mc_, ms_ = hcs[0], hcs[1]
                for c in range(2):
                    dstv = HB[c].v(slice(None), k)
                    build_B(mc_, ms_, c, dstv.bc(HB[c].ap[:, :, k, :]), three_d=True)
            self.memset("dve", Sel.v(), 0.0)
            for q in range(3):
                self.memset("dve", Sel.v(slice(q, q + 1), parts=(q * 32, q * 32 + 32)), 1.0)
            self.memset("dve", Sel.v(slice(3, 4), parts=(96, 128)), 1.0)
        if self.dbg:
            self.dump("Ec", Ec.v(), (16, 128), F32)
            self.dump("Es", Es.v(), (16, 128), F32)
            self.dump("rho", rho.v(), (16,), F32)
            self.dump("FG", FG.v(), (5, 3, 16), F32)
            self.dump("Bk00", Bk[0][0].v(), (4, 128), BF16)
            self.dump("Bk11", Bk[1][1].v(), (4, 128), BF16)
            self.dump("Ck00", Ck[0][0].v(), (16, 32), BF16)
            self.dump("Ck21", Ck[2][1].v(), (16, 32), BF16)

        mask4 = mask_b.v().bc(mask_b.ap.unsqueeze(1).broadcast_to([128, 4, 128]))
        ghn_b = ghn.bc(cstp.ap[:, CP_GHN:CP_GHN + 8].unsqueeze(2).broadcast_to([128, 8, 128]))
        O_A = O_BIG
        ASZ = NB * D * 4
        O_B = O_BIG + ASZ
        BSZ = self.arena_size - O_B

        class Bump:
            def __init__(s_, base, size):
                s_.base = base
                s_.o = base
                s_.end = base + size

            def __call__(s_, n):
                r = s_.o
                s_.o += (n + PAGE - 1) // PAGE * PAGE
                assert s_.o <= s_.end, (s_.o - s_.base, s_.end - s_.base)
                return r

        def ec_b(pr):
            v = Ec.v(pr)
            return v.bc(Ec.ap[:, pr, :].unsqueeze(1).broadcast_to([128, NB, 128]))

        def es_b(pr):
            v = Es.v(pr)
            return v.bc(Es.ap[:, pr, :].unsqueeze(1).broadcast_to([128, NB, 128]))

        def norm_to_hT(gc, gcol0, dst, xt_, xn_):
            ss = small[0]
            lnv = small[1]
            rstd = small[2]
            for blk in range(NB):
                self.act(xn_[blk % 2].v(), xt_.v(blk), AF.Square, accum=ss.v(slice(blk, blk + 1)))
            self.act(lnv.v(slice(0, NB)), ss.v(slice(0, NB)), AF.Ln, scale=1.0 / D, bias=1e-6)
            self.act(rstd.v(slice(0, NB)), lnv.v(slice(0, NB)), AF.Exp, scale=-0.5)
            for blk in range(NB):
                xb = xn_[blk % 2]
                self.act(xb.v(), xt_.v(blk), AF.Identity, scale=rstd.v(slice(blk, blk + 1)))
                for kq in range(4):
                    b = self.bank()
                    pb = self.psb(b)
                    for j in range(4):
                        kc = kq * 4 + j
                        self.tr(pb.v(slice(j * 128, (j + 1) * 128)), xb.v(slice(kc * 128, (kc + 1) * 128)))
                    gsl = gc.bc(cstp.ap[:, gcol0 + kq * 4:gcol0 + kq * 4 + 4].unsqueeze(2).broadcast_to([128, 4, 128]))
                    dv = dst.v(slice(kq * 4, kq * 4 + 4), slice(blk * 128, (blk + 1) * 128))
                    self.tt("dve", dv, ps3(pb, 512, 4), gsl, ALU.mult)

        def prep_gen(src_d, t0, stg_x, stg_n):
            xs = sb(stg_x, (D,), F32)
            xn_ = [sb(stg_n + i * D * 2, (D,), BF16) for i in range(2)]
            ss = small[6]
            l2 = small[7]
            for blk in range(NB):
                self.dma("sp", xs.v().ap, src_d[t0 + blk * 128:t0 + (blk + 1) * 128, :], reads=[], writes=[xs.v()])
                xb = xn_[blk % 2]
                self.act(xb.v(), xs.v(), AF.Square, accum=ss.v(slice(blk, blk + 1)))
                self.act(l2.v(slice(blk, blk + 1)), ss.v(slice(blk, blk + 1)), AF.Ln, scale=1.0 / D, bias=1e-6)
                self.act(l2.v(slice(8 + blk, 9 + blk)), l2.v(slice(blk, blk + 1)), AF.Exp, scale=-0.5)
                self.act(xb.v(), xs.v(), AF.Identity, scale=l2.v(slice(8 + blk, 9 + blk)))
                for kq in range(4):
                    b = self.bank()
                    pb = self.psb(b)
                    for j in range(4):
                        kc = kq * 4 + j
                        self.tr(pb.v(slice(j * 128, (j + 1) * 128)), xb.v(slice(kc * 128, (kc + 1) * 128)))
                    gsl = g1c.bc(cstp.ap[:, CP_G1 + kq * 4:CP_G1 + kq * 4 + 4].unsqueeze(2).broadcast_to([128, 4, 128]))
                    dv = hT.v(slice(kq * 4, kq * 4 + 4), slice(blk * 128, (blk + 1) * 128))
                    self.tt("dve", dv, ps3(pb, 512, 4), gsl, ALU.mult)
                    yield

        STG_WARM = (W_STGX, W_STGN)
        STG_MAIN = (O_MIX, O_MIX + D * 4)

        def tile(src_d, t0, warm, first_main, nxt):
            A = Bump(O_A, ASZ)
            B = Bump(O_B, BSZ)
            self.ck(3 if warm else 22)

            A = Bump(O_A, ASZ)
            B = Bump(O_B, BSZ)
            if warm:
                alow = sb(W_ALOW, (NT,), BF16)
                utok = sb(W_UTOK, (NB, 512), BF16)
                uT = uTb = oAT = oBT = None
            else:
                alow = sb(B(NT * 2), (NT,), BF16)
                uT = sb(B(4 * NT * 4), (4, NT), F32)
                uTb = sb(B(4 * NT * 2), (4, NT), BF16)
                oAT = sb(B(8 * NT * 2), (8, NT), BF16)
                oBT = sb(B(4 * NT * 2), (4, NT), BF16)
            B_REST = B.o
            e_t = sb(A(NT * 4), (NT,), F32)
            l_t = sb(A(NT * 4), (NT,), F32)
            s_t = sb(A(NT * 4), (NT,), F32)
            E1 = sb(A(NT * 4), (NT,), F32)
            E2 = sb(A(NT * 4), (NT,), F32)
            E3 = sb(A(NT * 4), (NT,), F32)
            qd = sb(A(4 * NT * 2), (4, NT), BF16)
            ki = sb(A(4 * NT * 2), (4, NT), BF16)
            ks = sb(A(4 * NT * 2), (4, NT), BF16)
            kst = sb(A(NB * 512 * 2), (NB, 512), BF16)
            sT = [sb(A(512 * 2), (512,), BF16) for _ in range(2)]
            junk = sb(A(256 * 2), (256,), BF16)
            nsl = sb(A(256), (4, NB), F32)
            dl = sb(A(256), (4, NB), F32)
            if warm:
                vt = sb(W_VT, (NB, 1024), BF16)
                srt = oa = None
            else:
                vt = sb(B(NB * 1024 * 2), (NB, 1024), BF16)
                srt = sb(B(NB * 1024 * 2), (NB, 1024), BF16)
                oa = [sb(B(1024 * 2), (1024,), BF16)]

            wal = self.wload("w_in", 0, 16, OFF_AL, 32)
            wblk = self.wload("w_in", 0, 16, OFF_U, 512)
            self.ck(3.1 if warm else 22.1)
            b = self.bank()
            pa = self.psf(b)
            for kc in range(16):
                self.mm(pa.v(slice(0, NT), parts=(0, 32)), wal.v(kc), hT.v(kc), start=(kc == 0), stop=(kc == 15))
            self.ck(3.2 if warm else 22.2)
            self.cp("act", alow.v(parts=(0, 32)), pa.v(slice(0, NT), parts=(0, 32)))
            self.ck(3.3 if warm else 22.3)
            if warm:
                for blk in range(NB):
                    b = self.bank()
                    pu = self.psf(b)
                    for kc in range(16):
                        self.mm(pu.v(), hT.v(kc, slice(blk * 128, (blk + 1) * 128)), wblk.v(kc), start=(kc == 0), stop=(kc == 15))
                    self.cp("act", utok.v(blk), pu.v())
            else:
                for ft in range(4):
                    b = self.bank()
                    pu = self.psf(b)
                    for kc in range(16):
                        self.mm(pu.v(slice(0, NT)), wblk.v(kc, slice(ft * 128, (ft + 1) * 128)), hT.v(kc),
                                start=(kc == 0), stop=(kc == 15))
                    self.cp("act", uT.v(ft), pu.v(slice(0, NT)))
                    self.cp("act", uTb.v(ft), pu.v(slice(0, NT)))

            self.ck(5 if warm else 24)

            def gla_rest():
                wq = None if warm else self.wload("w_in", 0, 16, OFF_Q, 512)
                wk = self.wload("w_in", 0, 16, OFF_K, 512)
                for h in range(4):
                    b = self.bank()
                    pz = self.psf(b)
                    self.mm(pz.v(slice(0, NT)), wa2_b.v(slice(h * 128, (h + 1) * 128), parts=(0, 32)), alow.v(parts=(0, 32)))
                    self.act(e_t.v(), pz.v(slice(0, NT)), AF.Exp, scale=-1.0, bias=nba2.v(slice(h, h + 1)))
                    self.act(l_t.v(), e_t.v(), AF.Ln, bias=1.0)
                    self.scan(s_t.v(), cmask, l_t.v(), 0.0)
                    if not warm:
                        self.act(E1.v(), s_t.v(), AF.Exp, scale=-1.0 / 16.0)
                        self.act(E2.v(), s_t.v(), AF.Exp, scale=1.0 / 16.0)
                    sl_v = s_t.v()
                    self.ts("dve", nsl.v(h), sl_v.bc(s_t.ap[:, 127::128]), -1.0 / 16.0, ALU.mult)
                    for c in range(NB):
                        self.act(E3.v(slice(c * 128, (c + 1) * 128)), s_t.v(slice(c * 128, (c + 1) * 128)), AF.Exp,
                                 scale=1.0 / 16.0, bias=nsl.v(h, slice(c, c + 1)))
                    self.act(dl.v(h), nsl.v(h), AF.Exp)
                    if not warm:
                        b = self.bank()
                        pq = self.psf(b)
                        for kc in range(16):
                            self.mm(pq.v(slice(0, NT)), wq.v(kc, slice(h * 128, (h + 1) * 128)), hT.v(kc), start=(kc == 0), stop=(kc == 15))
                            if kc == 7:
                                yield
                        self.stt(qd.v(h), pq.v(slice(0, NT)), 128.0 ** -0.5, E1.v(), ALU.mult, ALU.mult)
                        yield
                    b = self.bank()
                    pk = self.psf(b)
                    for kc in range(16):
                        self.mm(pk.v(slice(0, NT)), wk.v(kc, slice(h * 128, (h + 1) * 128)), hT.v(kc), start=(kc == 0), stop=(kc == 15))
                        if kc == 7:
                            yield
                    if not warm:
                        self.tt("dve", ki.v(h), pk.v(slice(0, NT)), E2.v(), ALU.mult)
                    self.tt("dve", ks.v(h), pk.v(slice(0, NT)), E3.v(), ALU.mult, extra_writes=[("hd", t0, warm, h)])
                    emit_casts(1, gate=("hd", t0, warm, h))
                    yield


                for cb in range(2):
                    wv = self.wload("w_in", 0, 16, OFF_V + cb * 512, 512)
                    for blk in range(NB):
                        b = self.bank()
                        pv_ = self.psf(b)
                        for kc in range(16):
                            self.mm(pv_.v(), hT.v(kc, slice(blk * 128, (blk + 1) * 128)), wv.v(kc), start=(kc == 0), stop=(kc == 15))
                            if kc == 7:
                                yield
                        lastb = (blk == NB - 1)
                        self.cp("act", vt.v(blk, slice(cb * 512, (cb + 1) * 512)), pv_.v(),
                                extra_writes=([("vd", t0, warm, cb)] if lastb else []))
                        if lastb:
                            emit_casts(1, gate=("vd", t0, warm, cb))
                        yield
                if warm and nxt is not None:
                    yield from prep_gen(nxt[0], nxt[1], STG_WARM[0], STG_WARM[1])
                if not warm:
                    for cb in range(2):
                        wr_ = self.wload("w_in", 0, 16, OFF_R + cb * 512, 512)
                        for blk in range(NB):
                            b = self.bank()
                            pr_ = self.psf(b)
                            for kc in range(16):
                                self.mm(pr_.v(), hT.v(kc, slice(blk * 128, (blk + 1) * 128)), wr_.v(kc), start=(kc == 0), stop=(kc == 15))
                                if kc == 7:
                                    yield
                            self.act(srt.v(blk, slice(cb * 512, (cb + 1) * 512)), pr_.v(), AF.Silu)
                            yield
                for blk in range(NB):
                    b = self.bank()
                    pb = self.psb(b)
                    for h in range(4):
                        self.tr(pb.v(slice(h * 128, (h + 1) * 128)), ks.v(h, slice(blk * 128, (blk + 1) * 128)))
                    self.cp("act", kst.v(blk), pb.v(slice(0, 512)))
                yield
                ss4 = small[3]
                ln4 = small[4]
                rs4 = small[5]
                for c in range(NB):
                    csl = slice(c * 128, (c + 1) * 128)
                    if not warm:
                        b = self.bank()
                        psc = self.psf(b)
                        for h in range(4):
                            self.mm(psc.v(slice(h * 128, (h + 1) * 128)), ki.v(h, csl), qd.v(h, csl))
                        sTc = sT[c % 2]
                        self.tt("dve", v3(sTc, 4), ps3(psc, 512, 4), mask4, ALU.mult)
                        pos = []
                        posb = []
                        for hp in range(2):
                            b = self.bank(hold=True)
                            posb.append(b)
                            po = self.psf(b)
                            pos.append(po)
                            for hh in range(2):
                                h = hp * 2 + hh
                                ov = po.v(slice(hh * 256, (hh + 1) * 256))
                                self.mm(ov, sTc.v(slice(h * 128, (h + 1) * 128)), vt.v(c, slice(h * 256, (h + 1) * 256)), start=True, stop=False)
                                self.mm(ov, qd.v(h, csl), state_b.v(h), start=False, stop=True)
                        for h in range(4):
                            ov = pos[h // 2].v(slice((h % 2) * 256, (h % 2 + 1) * 256))
                            self.act(junk.v(), ov, AF.Square, accum=ss4.v(slice(h, h + 1)))
                        self.act(ln4.v(slice(0, 4)), ss4.v(slice(0, 4)), AF.Ln, scale=1.0 / 256.0, bias=1e-6)
                        self.act(rs4.v(slice(0, 4)), ln4.v(slice(0, 4)), AF.Exp, scale=-0.5)
                        oac = oa[0]
                        for h in range(4):
                            ov = pos[h // 2].v(slice((h % 2) * 256, (h % 2 + 1) * 256))
                            self.stt(oac.v(slice(h * 256, (h + 1) * 256)), ov, rs4.v(slice(h, h + 1)),
                                     srt.v(c, slice(h * 256, (h + 1) * 256)), ALU.mult, ALU.mult)
                        for b in posb:
                            self.release(b)
                        b = self.bank()
                        pb = self.psb(b)
                        for j in range(8):
                            self.tr(pb.v(slice(j * 128, (j + 1) * 128)), oac.v(slice(j * 128, (j + 1) * 128)))
                        self.tt("dve", oAT.v(slice(None), csl), ps3(pb, 1024, 8), ghn_b, ALU.mult)
                        yield
                    last = warm and (t0 + NT == SEQH) and (c == NB - 1)
                    for hp in range(2):
                        b = self.bank()
                        pkv = self.psf(b)
                        for hh in range(2):
                            h = hp * 2 + hh
                            self.mm(pkv.v(slice(hh * 256, (hh + 1) * 256)), kst.v(c, slice(h * 128, (h + 1) * 128)),
                                    vt.v(c, slice(h * 256, (h + 1) * 256)))
                        for hh in range(2):
                            h = hp * 2 + hh
                            self.stt(state.v(h), state.v(h), dl.v(h, slice(c, c + 1)), pkv.v(slice(hh * 256, (hh + 1) * 256)),
                                     ALU.mult, ALU.add)
                            if (not warm) or last:
                                self.cp("act", state_b.v(h), state.v(h))
                    yield
                if self.dbg and first_main:
                    self.dump("oAT", oAT.v(), (8, NT), BF16)
                    self.dump("hT", hT.v(), (16, NT), BF16)
                    self.dump("uT", uT.v(), (4, NT), F32)
                    self.dump("qd", qd.v(), (4, NT), BF16)
                    self.dump("ks", ks.v(), (4, NT), BF16)
                    self.dump("state", state.v(), (4, 256), F32)

            s5a = [sb(O_MIX + i * NT * 4, (NT,), F32) for i in range(6)]
            wrr = [s5a[0], s5a[2]]
            wim = [s5a[1], s5a[3]]
            tP = [s5a[4], s5a[5]]
            pbf = [sb(O_MIX + 6 * NT * 4 + i * NT * 2, (NT,), BF16) for i in range(4)]
            if warm:
                hS = None
                ctmp = sb(A(256), (16, 2), F32)
            else:
                hS = sb(B(4 * NT * 2), (4, NT), BF16)
                ctmp = sb(B(256), (16, 2), F32)

            def s5_gen():
                def issue_A(pr):
                    ft = pr // 4
                    r0 = (pr % 4) * 32
                    bR = self.bank(hold=True)
                    pR = self.psf(bR)
                    bI = self.bank(hold=True)
                    pI = self.psf(bI)
                    for k in range(NB):
                        ksl = slice(k * 128, (k + 1) * 128)
                        self.mm(pR.v(ksl), Bk[k][0].v(ft, parts=(r0, r0 + 32)), uTb.v(ft, ksl, parts=(r0, r0 + 32)), tp=(r0, 0))
                    for k in range(NB):
                        ksl = slice(k * 128, (k + 1) * 128)
                        self.mm(pI.v(ksl), Bk[k][1].v(ft, parts=(r0, r0 + 32)), uTb.v(ft, ksl, parts=(r0, r0 + 32)), tp=(r0, 0))
                    return (bR, pR, bI, pI)
                nxt = issue_A(0)
                by = None
                py = None
                for pr in range(16):
                    ft = pr // 4
                    if (not warm) and pr % 4 == 0:
                        by = self.bank(hold=True)
                        py = self.psf(by)
                    bR, pR, bI, pI = nxt
                    wr_ = wrr[pr % 2]
                    wi_ = wim[pr % 2]
                    pR3 = ps3(pR, NT, NB)
                    pI3 = ps3(pI, NT, NB)
                    self.tt("dve", v3(tP[0]), pR3, ec_b(pr), ALU.mult)
                    self.tt("dve", v3(tP[1]), pI3, es_b(pr), ALU.mult)
                    self.tt("dve", wr_.v(), tP[0].v(), tP[1].v(), ALU.add)
                    self.tt("dve", v3(tP[0]), pI3, ec_b(pr), ALU.mult)
                    self.tt("dve", v3(tP[1]), pR3, es_b(pr), ALU.mult)
                    self.tt("dve", wi_.v(), tP[0].v(), tP[1].v(), ALU.subtract)
                    self.release(bR)
                    self.release(bI)
                    if pr + 1 < 16:
                        nxt = issue_A(pr + 1)
                    yield
                    rv = rho.v(slice(pr, pr + 1))
                    rb = rv.bc(rv.ap.broadcast_to([128, NT]))
                    self.scan(wr_.v(), rb, wr_.v(), carry.v(pr, slice(0, 1)))
                    self.scan(wi_.v(), rb, wi_.v(), carry.v(pr, slice(1, 2)))
                    wre = wr_.v(slice(NT - 1, NT))
                    wie = wi_.v(slice(NT - 1, NT))
                    gc_ = FG.v(4, 0, slice(pr, pr + 1))
                    gs_ = FG.v(4, 1, slice(pr, pr + 1))
                    ngs = FG.v(4, 2, slice(pr, pr + 1))
                    self.act(ctmp.v(pr, slice(0, 1)), wie, AF.Identity, scale=ngs)
                    self.act(ctmp.v(pr, slice(1, 2)), wie, AF.Identity, scale=gc_)
                    self.act(carry.v(pr, slice(0, 1)), wre, AF.Identity, scale=gc_, bias=ctmp.v(pr, slice(0, 1)))
                    self.act(carry.v(pr, slice(1, 2)), wre, AF.Identity, scale=gs_, bias=ctmp.v(pr, slice(1, 2)))
                    if warm:
                        yield
                        continue
                    m0 = (pr % 4) * 32
                    self.tt("dve", v3(pbf[0]), v3(wr_), ec_b(pr), ALU.mult)
                    self.stt(v3(pbf[1]), v3(wi_), -1.0, es_b(pr), ALU.mult, ALU.mult)
                    self.stt(v3(pbf[2]), v3(wr_), -1.0, es_b(pr), ALU.mult, ALU.mult)
                    self.stt(v3(pbf[3]), v3(wi_), -1.0, ec_b(pr), ALU.mult, ALU.mult)
                    for k in range(NB):
                        ksl = slice(k * 128, (k + 1) * 128)
                        ov = py.v(ksl, parts=(m0, m0 + 32))
                        self.mm(ov, Ck[k][0].v(pr), pbf[0].v(ksl), start=True, stop=False, tp=(0, m0))
                        self.mm(ov, Ck[k][0].v(pr), pbf[1].v(ksl), start=False, stop=False, tp=(0, m0))
                        self.mm(ov, Ck[k][1].v(pr), pbf[2].v(ksl), start=False, stop=False, tp=(0, m0))
                        self.mm(ov, Ck[k][1].v(pr), pbf[3].v(ksl), start=False, stop=True, tp=(0, m0))
                    yield
                    if pr % 4 != 3:
                        continue
                    yb, y2b, sgq = tP[0], tP[1], wrr[pr % 2]
                    self.stt(yb.v(), uT.v(ft), ds5(ft), py.v(slice(0, NT)), ALU.mult, ALU.add)
                    self.release(by)
                    self.act(y2b.v(), yb.v(), AF.Square)
                    self.ts("dve", y2b.v(), y2b.v(), 0.044715, ALU.mult, 1.0, ALU.add)
                    self.tt("dve", y2b.v(), y2b.v(), yb.v(), ALU.mult)
                    self.act(sgq.v(), y2b.v(), AF.Sigmoid, scale=2.0 * math.sqrt(2.0 / math.pi))
                    self.tt("dve", hS.v(ft), yb.v(), sgq.v(), ALU.mult)
                    yield

            def s5_warm_gen():
                Pr, Pi, P1, P2 = s5a[0], s5a[1], s5a[2], s5a[3]
                sm = small[3].v(slice(32, 64))
                smb = small[3]
                for ft in range(4):
                    bZR = self.bank(hold=True)
                    ZR = self.psf(bZR)
                    bZI = self.bank(hold=True)
                    ZI = self.psf(bZI)
                    for q in range(4):
                        pr = ft * 4 + q
                        r0 = q * 32
                        for k in range(NB):
                            ksl = slice(k * 128, (k + 1) * 128)
                            lh = utok.v(k, slice(pr * 32, (pr + 1) * 32))
                            self.mm(ZR.v(ksl, parts=(r0, r0 + 32)), lh, TT0[0].v(pr), tp=(0, r0))
                            self.mm(ZI.v(ksl, parts=(r0, r0 + 32)), lh, TT0[1].v(pr), tp=(0, r0))
                    yield
                    hbr = HB[0].v(ft).bc(HB[0].ap[:, ft, :, :].rearrange("p a b -> p (a b)"))
                    hbi = HB[1].v(ft).bc(HB[1].ap[:, ft, :, :].rearrange("p a b -> p (a b)"))
                    self.tt("dve", P1.v(), ZR.v(), hbr, ALU.mult)
                    self.tt("dve", P2.v(), ZI.v(), hbi, ALU.mult)
                    self.tt("dve", Pr.v(), P1.v(), P2.v(), ALU.subtract)
                    self.tt("dve", P1.v(), ZR.v(), hbi, ALU.mult)
                    self.tt("dve", P2.v(), ZI.v(), hbr, ALU.mult)
                    self.tt("dve", Pi.v(), P1.v(), P2.v(), ALU.add)
                    self.release(bZR)
                    self.release(bZI)
                    yield
                    bS = self.bank()
                    pS = self.psf(bS)
                    for k in range(NB):
                        ksl = slice(k * 128, (k + 1) * 128)
                        self.mm(pS.v(slice(0, 4)), Pr.v(ksl), Sel.v(), start=(k == 0), stop=(k == NB - 1))
                    for k in range(NB):
                        ksl = slice(k * 128, (k + 1) * 128)
                        self.mm(pS.v(slice(8, 12)), Pi.v(ksl), Sel.v(), start=(k == 0), stop=(k == NB - 1))
                    psl = slice(ft * 4, ft * 4 + 4)
                    cr_v = carry.v(psl).bc(carry.ap[:, ft * 4:ft * 4 + 4, 0])
                    ci_v = carry.v(psl).bc(carry.ap[:, ft * 4:ft * 4 + 4, 1])
                    r5 = R512.v(psl)
                    gc4 = FG.v(4, 0, psl)
                    gs4 = FG.v(4, 1, psl)
                    w_r = smb.v(slice(32, 36))
                    w_i = smb.v(slice(36, 40))
                    ta_ = smb.v(slice(40, 44))
                    tb2 = smb.v(slice(44, 48))
                    self.tt("dve", ta_, cr_v, r5, ALU.mult)
                    self.tt("dve", w_r, ta_, pS.v(slice(0, 4)), ALU.add)
                    self.tt("dve", tb2, ci_v, r5, ALU.mult)
                    self.tt("dve", w_i, tb2, pS.v(slice(8, 12)), ALU.add)
                    self.tt("dve", ta_, w_r, gc4, ALU.mult)
                    self.tt("dve", tb2, w_i, gs4, ALU.mult)
                    self.tt("dve", cr_v, ta_, tb2, ALU.subtract)
                    self.tt("dve", ta_, w_r, gs4, ALU.mult)
                    self.tt("dve", tb2, w_i, gc4, ALU.mult)
                    self.tt("dve", ci_v, ta_, tb2, ALU.add)
                    yield

            g1_ = gla_rest()
            g2_ = s5_warm_gen() if warm else s5_gen()
            alive1 = alive2 = True
            self.ck(8 if warm else 27)
            while alive1 or alive2:
                if alive1:
                    try:
                        next(g1_)
                    except StopIteration:
                        alive1 = False
                if alive2:
                    try:
                        next(g2_)
                    except StopIteration:
                        alive2 = False
            sgb = pbf[0]
            if warm:
                self.act(ctmp.v(0, slice(0, 1)), carry.v(15, slice(1, 2)), AF.Identity, extra_writes=[("wdone", t0)])
                return
            self.ck(28)
            A = Bump(O_A, ASZ)
            xt2 = sb(A(NB * D * 4), (NB, D), F32)
            xin2 = xt2.v()
            self.dma("sp", xin2.ap, src_d[t0:t0 + NT, :].rearrange("(b p) d -> p b d", p=128), reads=[], writes=[xin2])
            wg = self.wload("w_glu", 0, 4, 0, 512)
            for fo in range(4):
                b = self.bank()
                pg = self.psf(b)
                for kc in range(4):
                    self.mm(pg.v(slice(0, NT)), wg.v(kc, slice(fo * 128, (fo + 1) * 128)), hS.v(kc), start=(kc == 0), stop=(kc == 3))
                self.act(sgb.v(), pg.v(slice(0, NT)), AF.Sigmoid, bias=bglu(fo))
                self.tt("dve", oBT.v(fo), hS.v(fo), sgb.v(), ALU.mult)
            if self.dbg and first_main:
                self.dump("oBT", oBT.v(), (4, NT), BF16)

            self.ck(29)
            B = Bump(B_REST, O_B + BSZ - B_REST)
            hS_keep = B(4 * NT * 2)
            sa = [sb(B(NT * 2), (NT,), BF16) for _ in range(2)]
            sbg = [sb(B(NT * 2), (NT,), BF16) for _ in range(2)]
            m1 = [sb(B(NT * 4), (NT,), F32) for _ in range(2)]
            m2 = [sb(B(NT * 4), (NT,), F32) for _ in range(2)]
            for q in range(4):
                wga = self.wload("w_in", 0, 16, OFF_GA + q * 512, 512)
                wgb = self.wload("w_in", 0, 16, OFF_GB + q * 512, 512)
                sl = self.slot_rr
                self.slot_rr = (sl + 1) % 3
                wba = self.sb(self.O_W + sl * self.SLOT, (8, 512), BF16)
                wbb = self.sb(self.O_W + sl * self.SLOT + 8192, (4, 512), BF16)
                wsrc, wpl = self.wb["w_ba"]
                self.dma("sp", wba.v().ap, wsrc[:, q * 512:(q + 1) * 512].rearrange("(kc p) n -> p kc n", p=128),
                         reads=[k for (_, _, _, _, k) in wpl], writes=[wba.v()])
                wsrc, wpl = self.wb["w_bb"]
                self.dma("sp", wbb.v().ap, wsrc[:, q * 512:(q + 1) * 512].rearrange("(kc p) n -> p kc n", p=128),
                         reads=[k for (_, _, _, _, k) in wpl], writes=[wbb.v()])
                for jj in range(4):
                    j = q * 4 + jj
                    bA = self.bank()
                    pgA = self.psf(bA)
                    for kc in range(16):
                        self.mm(pgA.v(slice(0, NT)), wga.v(kc, slice(jj * 128, (jj + 1) * 128)), hT.v(kc), start=(kc == 0), stop=(kc == 15))
                    bB = self.bank()
                    pgB = self.psf(bB)
                    for kc in range(16):
                        self.mm(pgB.v(slice(0, NT)), wgb.v(kc, slice(jj * 128, (jj + 1) * 128)), hT.v(kc), start=(kc == 0), stop=(kc == 15))
                    bPA = self.bank()
                    pPA = self.psf(bPA)
                    for kc in range(8):
                        self.mm(pPA.v(slice(0, NT)), wba.v(kc, slice(jj * 128, (jj + 1) * 128)), oAT.v(kc), start=(kc == 0), stop=(kc == 7))
                    bPB = self.bank()
                    pPB = self.psf(bPB)
                    for kc in range(4):
                        self.mm(pPB.v(slice(0, NT)), wbb.v(kc, slice(jj * 128, (jj + 1) * 128)), oBT.v(kc), start=(kc == 0), stop=(kc == 3))
                    i2 = j % 2
                    self.act(sa[i2].v(), pgA.v(slice(0, NT)), AF.Sigmoid)
                    self.act(sbg[i2].v(), pgB.v(slice(0, NT)), AF.Sigmoid)
                    self.tt("dve", m1[i2].v(), pPA.v(slice(0, NT)), sa[i2].v(), ALU.mult)
                    self.tt("dve", m2[i2].v(), pPB.v(slice(0, NT)), sbg[i2].v(), ALU.mult)
                    self.tt("pool", mixT.v(j), m1[i2].v(), m2[i2].v(), ALU.add)
            if self.dbg and first_main:
                self.dump("mixT", mixT.v(), (16, NT), BF16)

            self.ck(30)
            for cb in range(4):
                wo = self.wload("w_out", 0, 16, cb * 512, 512)
                for blk in range(NB):
                    b = self.bank()
                    po = self.psf(b)
                    for kc in range(16):
                        self.mm(po.v(), mixT.v(kc, slice(blk * 128, (blk + 1) * 128)), wo.v(kc), start=(kc == 0), stop=(kc == 15))
                    xv = xt2.v(blk, slice(cb * 512, (cb + 1) * 512))
                    self.tt("dve", xv, xv, po.v(), ALU.add)
            if self.dbg and first_main:
                self.dump("x1", xt2.v(), (NB, D), F32)
            self.ck(31)
            B = Bump(O_B, BSZ)
            hid = sb(B(44 * NT * 2), (44, NT), BF16)
            sg2 = [sb(B(NT * 4), (NT,), F32) for _ in range(2)]
            xn2 = [sb(O_B + i * D * 2, (D,), BF16) for i in range(2)]
            norm_to_hT(g2c, CP_G2, hT, xt2, xn2)
            self.ck(32)
            for q in range(11):
                wg_ = self.wload("w_ffi", 0, 16, q * 512, 512)
                wu_ = self.wload("w_ffi", 0, 16, DFF + q * 512, 512)
                for jj in range(4):
                    c = q * 4 + jj
                    bG = self.bank()
                    pG = self.psf(bG)
                    for kc in range(16):
                        self.mm(pG.v(slice(0, NT)), wg_.v(kc, slice(jj * 128, (jj + 1) * 128)), hT.v(kc), start=(kc == 0), stop=(kc == 15))
                    bU = self.bank()
                    pU = self.psf(bU)
                    for kc in range(16):
                        self.mm(pU.v(slice(0, NT)), wu_.v(kc, slice(jj * 128, (jj + 1) * 128)), hT.v(kc), start=(kc == 0), stop=(kc == 15))
                    self.act(sg2[c % 2].v(), pG.v(slice(0, NT)), AF.Silu)
                    self.tt("dve", hid.v(c), pU.v(slice(0, NT)), sg2[c % 2].v(), ALU.mult)
            self.ck(33)
            def ffo_gen():
                for cb in range(4):
                    banks = [self.bank(hold=True) for _ in range(NB)]
                    for q in range(4):
                        wo_ = self.wload("w_ffo", q * 11, 11, cb * 512, 512)
                        for blk in range(NB):
                            po = self.psf(banks[blk])
                            for kk in range(11):
                                kc = q * 11 + kk
                                self.mm(po.v(), hid.v(kc, slice(blk * 128, (blk + 1) * 128)), wo_.v(kk),
                                        start=(kc == 0), stop=(kc == 43))
                        yield
                    for blk in range(NB):
                        po = self.psf(banks[blk])
                        xv = xt2.v(blk, slice(cb * 512, (cb + 1) * 512))
                        self.tt("dve", xv, xv, po.v(), ALU.add)
                        self.release(banks[blk])
            ga_ = ffo_gen()
            gb_ = prep_gen(nxt[0], nxt[1], STG_MAIN[0], STG_MAIN[1]) if nxt is not None else iter(())
            la = lb = True
            while la or lb:
                if la:
                    try:
                        next(ga_)
                    except StopIteration:
                        la = False
                if lb:
                    try:
                        next(gb_)
                    except StopIteration:
                        lb = False
            self.ck(34)
            ss = small[0]
            lnv = small[1]
            rstd = small[2]
            for blk in range(NB):
                self.act(xn2[blk % 2].v(), xt2.v(blk), AF.Square, accum=ss.v(slice(blk, blk + 1)))
            self.act(lnv.v(slice(0, NB)), ss.v(slice(0, NB)), AF.Ln, scale=1.0 / D, bias=1e-6)
            self.act(rstd.v(slice(0, NB)), lnv.v(slice(0, NB)), AF.Exp, scale=-0.5)
            gslot = self.wslot(16, 512)
            gfv = View(gslot.ap.rearrange("p a b -> p (a b)").bitcast(F32)[:, 0:D], gslot.v().keys)
            self.dma("sp", gfv.ap, gfb_d[:, :], reads=[], writes=[gfv])
            for blk in range(NB):
                self.stt(xt2.v(blk), xt2.v(blk), rstd.v(slice(blk, blk + 1)), gfv, ALU.mult, ALU.mult)
            key = ("out", t0)
            self.dma("pool", out_d[t0:t0 + NT, :].rearrange("(b p) d -> p b d", p=128), xt2.v().ap, reads=[xt2.v()], writes=[key])
            self.final_reads.append(key)

        try:
            self.ck(1)
            tiles = [(xp_d, SEQH - (self.n_warm - i) * NT, True) for i in range(self.n_warm)]
            tiles += [(x_d, i * NT, False) for i in range(self.n_main)]
            if tiles:
                stg = STG_WARM if tiles[0][2] else STG_MAIN
                for _ in prep_gen(tiles[0][0], tiles[0][1], stg[0], stg[1]):
                    pass
            seen_main = False
            for i, (sd, t0, wm) in enumerate(tiles):
                nxt = (tiles[i + 1][0], tiles[i + 1][1]) if i + 1 < len(tiles) else None
                if not wm and not seen_main:
                    emit_casts(10 ** 6, gate=(("wdone", tiles[i - 1][1]) if i > 0 else None))
                    self.ck(20)
                tile(sd, t0, wm, (not wm) and (not seen_main), nxt)
                if wm:
                    emit_casts(1, gate=("wdone", t0))
                else:
                    seen_main = True
            emit_casts(10 ** 6)
        except StopBuild:
            pass
        self.P.op("sp", lambda h: h.nop(), reads=list(self.final_reads))

        P.analyze()
        sems = {"pe": sem_ctx["s_pe"], "act": sem_ctx["s_act"], "dve": sem_ctx["s_dve"],
                "pool": sem_ctx["s_pool"], "sp": sem_ctx["s_sp"]}
        dma_sems = {"sp": [sem_ctx["dsp%d" % i] for i in range(6)], "pool": [sem_ctx["dpl%d" % i] for i in range(4)],
                    "cast": [sem_ctx["dca%d" % i] for i in range(2)]}
        P.plan(sems, dma_sems)
        blk_ctx = nc.Block()
        block = blk_ctx.__enter__()

        @block.tensor
        def _(e):
            P.emit_engine("pe", e)

        @block.scalar
        def _(e):
            P.emit_engine("act", e)

        @block.vector
        def _(e):
            P.emit_engine("dve", e)

        @block.gpsimd
        def _(e):
            P.emit_engine("pool", e)

        @block.sync
        def _(e):
            P.emit_engine("sp", e)
        blk_ctx.__exit__(None, None, None)
        for c in reversed(ctx):
            c.__exit__(None, None, None)
        return nc


def build_program(n_warm=SEQH // NT, n_main=SEQH // NT, dbg=False, stop_at=10 ** 9):
    nc = bass.Bass("TRN2", target_bir_lowering=False)
    bld = Builder(nc, n_warm, n_main, dbg)
    bld.stop_at = stop_at
    bld.build()
    return nc, bld


def host_consts(inp):
    f = np.float32
    cstp = np.zeros((128, CP_N), f)
    cm = np.ones(NT, f)
    cm[0::128] = 0.0
    cstp[:, CP_CMASK:CP_CMASK + NT] = cm[None, :]
    cstp[:, CP_G1:CP_G1 + 16] = np.asarray(inp["norm1_g"], f).reshape(16, 128).T
    cstp[:, CP_G2:CP_G2 + 16] = np.asarray(inp["norm2_g"], f).reshape(16, 128).T
    cstp[:, CP_GHN:CP_GHN + 8] = np.asarray(inp["gla_norm_g"], f).reshape(8, 128).T
    cstp[:, CP_BA2:CP_BA2 + 4] = np.asarray(inp["b_a2"], f).reshape(4, 128).T
    cstp[:, CP_DS5:CP_DS5 + 4] = np.asarray(inp["s5_d"], f).reshape(4, 128).T
    cstp[:, CP_BGLU:CP_BGLU + 4] = np.asarray(inp["b_glu"], f).reshape(4, 128).T
    cstt = np.zeros((128, CT_N), f)
    cstt[:, CT_ID:CT_ID + 128] = np.eye(128, dtype=f)
    cstt[:, CT_MASK:CT_MASK + 128] = np.triu(np.ones((128, 128), f))
    cstt[0:16, CT_WA2:CT_WA2 + 512] = np.asarray(inp["w_a2"], f).reshape(16, 512)
    lre = np.asarray(inp["lam_re"], f).reshape(16, 2, 64)
    lim = np.asarray(inp["lam_im"], f).reshape(16, 2, 64)
    ldt = np.asarray(inp["log_dt"], f).reshape(16, 2)
    cstt[:, CT_LRE:CT_LRE + 16] = lre.transpose(1, 2, 0).reshape(128, 16)
    cstt[:, CT_LIM:CT_LIM + 16] = lim.transpose(1, 2, 0).reshape(128, 16)
    cstt[:, CT_LDT:CT_LDT + 16] = np.broadcast_to(ldt.T[:, None, :], (2, 64, 16)).reshape(128, 16)
    bre = np.asarray(inp["s5_b_re"], f).reshape(32, 64, 16)
    bim = np.asarray(inp["s5_b_im"], f).reshape(32, 64, 16)
    cre = np.asarray(inp["s5_c_re"], f).reshape(32, 16, 64)
    cim = np.asarray(inp["s5_c_im"], f).reshape(32, 16, 64)
    Bl_re = np.zeros((128, 16, 16), f)
    Bl_im = np.zeros((128, 16, 16), f)
    Cl_re = np.zeros((128, 16, 32), f)
    Cl_im = np.zeros((128, 16, 32), f)
    for pr in range(16):
        ft = pr // 4
        r0 = (pr % 4) * 32
        for g2 in range(2):
            g = 2 * pr + g2
            Bl_re[g2 * 64:(g2 + 1) * 64, pr, :] = bre[g]
            Bl_im[g2 * 64:(g2 + 1) * 64, pr, :] = bim[g]
            Cl_re[g2 * 64:(g2 + 1) * 64, pr, g2 * 16:(g2 + 1) * 16] = cre[g].T
            Cl_im[g2 * 64:(g2 + 1) * 64, pr, g2 * 16:(g2 + 1) * 16] = cim[g].T
    cstt[:, CT_BRE:CT_BRE + 256] = Bl_re.reshape(128, 256)
    cstt[:, CT_BIM:CT_BIM + 256] = Bl_im.reshape(128, 256)
    cstt[:, CT_CRE:CT_CRE + 512] = Cl_re.reshape(128, 512)
    cstt[:, CT_CIM:CT_CIM + 512] = Cl_im.reshape(128, 512)
    return cstp, cstt


def make_in_maps(inp):
    f = np.float32
    x = np.asarray(inp["x"], f)
    cstp, cstt = host_consts(inp)
    shared = {
        "cstp": cstp, "cstt": cstt,
        "gfb": np.ascontiguousarray(np.broadcast_to(np.asarray(inp["final_norm_g"], f).reshape(1, D), (128, D))),
        "w_in": np.ascontiguousarray(np.asarray(inp["w_in"], f).reshape(D, DIN)),
        "w_ba": np.ascontiguousarray(np.asarray(inp["w_branch_a"], f).reshape(1024, D)),
        "w_bb": np.ascontiguousarray(np.asarray(inp["w_branch_b"], f).reshape(512, D)),
        "w_out": np.ascontiguousarray(np.asarray(inp["w_out"], f).reshape(D, D)),
        "w_glu": np.ascontiguousarray(np.asarray(inp["w_glu"], f).reshape(512, 512)),
        "w_ffi": np.ascontiguousarray(np.asarray(inp["w_ffn_in"], f).reshape(D, 2 * DFF)),
        "w_ffo": np.ascontiguousarray(np.asarray(inp["w_ffn_out"], f).reshape(DFF, D)),
    }
    zeros = np.zeros((SEQH, D), f)
    in_maps = []
    for core in range(8):
        b, half = core // 2, core % 2
        m = dict(shared)
        m["x"] = np.ascontiguousarray(x[b, half * SEQH:(half + 1) * SEQH])
        m["xp"] = np.ascontiguousarray(x[b, 0:SEQH]) if half == 1 else zeros
        in_maps.append(m)
    return in_maps


_CACHE = {}


def kernel(**inputs):
    in_maps = make_in_maps(inputs)
    if "nc" not in _CACHE:
        _CACHE["nc"] = build_program()[0]
    nc = _CACHE["nc"]
    res = run_bass_kernel_spmd(nc, in_maps, core_ids=list(range(8)))
    out = np.zeros((4, 2 * SEQH, D), np.float32)
    for core in range(8):
        b, half = core // 2, core % 2
        out[b, half * SEQH:(half + 1) * SEQH] = res.results[core]["out"]
    return out
```
